# Optimizing a Trainium2 kernel written in Bass

```python
import math
import jax, jax.numpy as jnp
from jax import lax
import numpy as np

D_MODEL = 1024
BATCH = 4
SEQ = 4096
DEPTH = 2

HEAD_DIM = 64
N_HEADS_A = D_MODEL // (2 * HEAD_DIM)
N_HEADS_B = D_MODEL // (2 * HEAD_DIM)
N_KV_B = max(1, N_HEADS_B // 4)
GROUP_B = N_HEADS_B // N_KV_B
MIX_WIDTH = (N_HEADS_A + N_HEADS_B) * HEAD_DIM
DILATED_CONFIGS = ((128, 1), (512, 4), (2048, 16))
SWA_RADIUS = 128
N_BUCKETS = 32
MAX_DISTANCE = 1024
D_FF = ((8 * D_MODEL // 3 + 127) // 128) * 128
PLE_DIM = 256
EPS = 1e-6
QBLOCK = 128
NEG = -1e30

QKV_WIDTHS = (N_HEADS_A * HEAD_DIM, N_HEADS_A * HEAD_DIM, N_HEADS_A * HEAD_DIM,
              N_HEADS_B * HEAD_DIM, N_KV_B * HEAD_DIM, N_KV_B * HEAD_DIM)
QKV_WIDTH = sum(QKV_WIDTHS)

kernel_name = "hybrid_dilated_swa_macaron_encoder"


def rms_norm(x, g):
    xf = x.astype(jnp.float32)
    y = xf * lax.rsqrt(jnp.mean(xf * xf, axis=-1, keepdims=True) + EPS) * g.astype(jnp.float32)
    return y.astype(x.dtype)


def swiglu(h, w_in, w_out):
    gate, up = jnp.split(h @ w_in, 2, axis=-1)
    return (jax.nn.silu(gate) * up) @ w_out


def t5_bucket(rel):
    half = N_BUCKETS // 2
    max_exact = half // 2
    ret = jnp.where(rel > 0, half, 0)
    n = jnp.abs(rel)
    nf = jnp.maximum(n, 1).astype(jnp.float32)
    large = max_exact + (jnp.log(nf / max_exact) / math.log(MAX_DISTANCE / max_exact)
                         * (half - max_exact)).astype(jnp.int32)
    large = jnp.minimum(large, half - 1)
    return ret + jnp.where(n < max_exact, n, large)


def banded_attention(q, k, v, radius, dilation, bias_heads, sink=None):
    N, Hk, G, L, E = q.shape
    bq = math.gcd(L, QBLOCK)
    nb = L // bq
    W = bq + 2 * radius
    pad = ((0, 0), (0, 0), (radius, radius), (0, 0))
    idx = (jnp.arange(nb) * bq)[:, None] + jnp.arange(W)[None, :]
    kb = jnp.pad(k, pad)[:, :, idx]
    vb = jnp.pad(v, pad)[:, :, idx]
    qb = q.reshape(N, Hk, G, nb, bq, E)
    logits = jnp.einsum('nhgbqe,nhbke->nhgbqk', qb, kb,
                        preferred_element_type=jnp.float32) * (E ** -0.5)
    rel = jnp.arange(W)[None, :] - radius - jnp.arange(bq)[:, None]
    bias = bias_heads[t5_bucket(rel * dilation)].astype(jnp.float32)
    bias = jnp.transpose(bias.reshape(bq, W, Hk, G), (2, 3, 0, 1))[:, :, None]
    in_band = jnp.abs(rel) <= radius
    in_seq = (idx >= radius) & (idx < radius + L)
    valid = in_band[None] & in_seq[:, None, :]
    logits = jnp.where(valid, logits + bias, NEG)
    m = jnp.max(logits, axis=-1)
    if sink is not None:
        sink_b = sink.astype(jnp.float32)[None, :, :, None, None]
        m = jnp.maximum(m, sink_b)
    pr = jnp.exp(logits - m[..., None])
    denom = jnp.sum(pr, axis=-1)
    if sink is not None:
        denom = denom + jnp.exp(sink_b - m)
    out = jnp.einsum('nhgbqk,nhbke->nhgbqe', pr.astype(v.dtype), vb,
                     preferred_element_type=jnp.float32) / denom[..., None]
    lse = m + jnp.log(denom)
    return out.reshape(N, Hk, G, L, E).astype(q.dtype), lse.reshape(N, Hk, G, L)


def dilated_attention(q, k, v, bias_heads):
    B, H, S, E = q.shape
    outs, lses = [], []
    for window, d in DILATED_CONFIGS:
        L = S // d
        def split(t):
            return t.reshape(B, H, L, d, E).transpose(0, 3, 1, 2, 4).reshape(B * d, H, L, E)
        o, lse = banded_attention(split(q)[:, :, None], split(k), split(v),
                                  window // (2 * d), d, bias_heads)
        outs.append(o[:, :, 0].reshape(B, d, H, L, E).transpose(0, 2, 3, 1, 4).reshape(B, H, S, E))
        lses.append(lse[:, :, 0].reshape(B, d, H, L).transpose(0, 2, 3, 1).reshape(B, H, S))
    wts = jax.nn.softmax(jnp.stack(lses), axis=0)
    return jnp.einsum('cbhs,cbhse->bhse', wts, jnp.stack(outs).astype(jnp.float32)).astype(q.dtype)


def setup_inputs(seed: int = 0) -> dict:
    key = jax.random.key(seed)
    ks = jax.random.split(key, 20)
    f32 = jnp.float32

    def nrm(k, shape):
        return jax.random.normal(k, shape, f32)

    def w(k, shape, fan_in):
        return nrm(k, shape) * fan_in ** -0.5

    def gain(k, shape):
        return 1.0 + 0.05 * nrm(k, shape)

    return {
        "x": nrm(ks[0], (BATCH, SEQ, D_MODEL)),
        "p": nrm(ks[1], (DEPTH, BATCH, SEQ, PLE_DIM)),
        "rel_bias": 0.5 * nrm(ks[2], (N_BUCKETS, N_HEADS_A + N_HEADS_B)),
        "norm_ffn1": gain(ks[3], (DEPTH, D_MODEL)),
        "ffn1_w_in": w(ks[4], (DEPTH, D_MODEL, 2 * D_FF), D_MODEL),
        "ffn1_w_out": w(ks[5], (DEPTH, D_FF, D_MODEL), D_FF),
        "norm_mix": gain(ks[6], (DEPTH, D_MODEL)),
        "w_qkv": w(ks[7], (DEPTH, D_MODEL, QKV_WIDTH), D_MODEL),
        "q_norm_a": gain(ks[8], (DEPTH, HEAD_DIM)),
        "k_norm_a": gain(ks[9], (DEPTH, HEAD_DIM)),
        "q_norm_b": gain(ks[10], (DEPTH, HEAD_DIM)),
        "k_norm_b": gain(ks[11], (DEPTH, HEAD_DIM)),
        "sink_b": 0.5 * nrm(ks[12], (DEPTH, N_HEADS_B)),
        "w_o": w(ks[13], (DEPTH, MIX_WIDTH, D_MODEL), MIX_WIDTH),
        "norm_ffn2": gain(ks[14], (DEPTH, D_MODEL)),
        "ffn2_w_in": w(ks[15], (DEPTH, D_MODEL, 2 * D_FF), D_MODEL),
        "ffn2_w_out": w(ks[16], (DEPTH, D_FF, D_MODEL), D_FF),
        "norm_ple": gain(ks[17], (DEPTH, D_MODEL)),
        "w_ple_gate": w(ks[18], (DEPTH, D_MODEL, D_MODEL), D_MODEL),
        "w_ple_proj": w(ks[19], (DEPTH, PLE_DIM, D_MODEL), PLE_DIM),
    }


def reference(x, p, rel_bias, norm_ffn1, ffn1_w_in, ffn1_w_out, norm_mix, w_qkv,
              q_norm_a, k_norm_a, q_norm_b, k_norm_b, sink_b, w_o, norm_ffn2,
              ffn2_w_in, ffn2_w_out, norm_ple, w_ple_gate, w_ple_proj):
    B, S, _ = x.shape
    split_at = [int(c) for c in np.cumsum(QKV_WIDTHS)[:-1]]
    bias_a = rel_bias[:, :N_HEADS_A]
    bias_b = rel_bias[:, N_HEADS_A:]

    def heads(t, n):
        return t.reshape(B, S, n, HEAD_DIM).transpose(0, 2, 1, 3)

    for i in range(DEPTH):
        x = x + 0.5 * swiglu(rms_norm(x, norm_ffn1[i]), ffn1_w_in[i], ffn1_w_out[i])

        h = rms_norm(x, norm_mix[i])
        qa, ka, va, qb, kb, vb = jnp.split(h @ w_qkv[i], split_at, axis=-1)

        qa = rms_norm(heads(qa, N_HEADS_A), q_norm_a[i])
        ka = rms_norm(heads(ka, N_HEADS_A), k_norm_a[i])
        oa = dilated_attention(qa, ka, heads(va, N_HEADS_A), bias_a)

        qb = rms_norm(heads(qb, N_HEADS_B), q_norm_b[i]).reshape(B, N_KV_B, GROUP_B, S, HEAD_DIM)
        kb = rms_norm(heads(kb, N_KV_B), k_norm_b[i])
        ob, _ = banded_attention(qb, kb, heads(vb, N_KV_B), SWA_RADIUS, 1, bias_b,
                                 sink_b[i].reshape(N_KV_B, GROUP_B))
        ob = ob.reshape(B, N_HEADS_B, S, HEAD_DIM)

        o = jnp.concatenate([oa, ob], axis=1).transpose(0, 2, 1, 3).reshape(B, S, MIX_WIDTH)
        x = x + o @ w_o[i]

        x = x + 0.5 * swiglu(rms_norm(x, norm_ffn2[i]), ffn2_w_in[i], ffn2_w_out[i])

        gate = jax.nn.sigmoid(rms_norm(x, norm_ple[i]) @ w_ple_gate[i])
        x = x + gate * (p[i] @ w_ple_proj[i])
    return x
```

```python
import contextlib
import numpy as np
import ml_dtypes
import concourse.bass as bass
import concourse.mybir as mybir
from concourse.bass_utils import run_bass_kernel_spmd

F32 = mybir.dt.float32
BF16 = mybir.dt.bfloat16
ALU = mybir.AluOpType
AF = mybir.ActivationFunctionType

D = 1024
T = 2048
NTG = 4
DFF = 2816
NFC = 22
HALO = 1024
FR = T + 2 * HALO
QKVW = 2432
NEGB = -30000.0
EPS = 1e-6
DIL = (1, 4, 16)
NCORES = 8

V_NF1, V_NMIX, V_NF2, V_NPLE = 0, 8, 16, 24
V_QNA, V_KNA, V_QNB, V_KNB = 32, 33, 34, 35
V_SINK = 36
VL = 44


class Prog:
    ENG = ('pe', 'act', 'dve', 'pool', 'sp')

    def __init__(self, nc):
        self.nc = nc
        self.ops = []
        self.res = {}
        self.pending_barrier = None

    def op(self, eng, fn, reads=(), writes=(), dma=None, final=False, also_writes=()):
        idx = len(self.ops)
        deps = set()
        for w in also_writes:
            st = self.res.setdefault(w, [None, []])
            for r in st[1]:
                if r != idx:
                    deps.add((r, True))
            st[0] = idx
            st[1] = []
        for r in reads:
            st = self.res.setdefault(r, [None, []])
            if st[0] is not None:
                deps.add((st[0], False))
            st[1].append(idx)
        for w in writes:
            st = self.res.setdefault(w, [None, []])
            if st[0] is not None:
                deps.add((st[0], False))
            for r in st[1]:
                if r != idx:
                    deps.add((r, True))
            st[0] = idx
            st[1] = []
        self.ops.append({'eng': eng, 'fn': fn, 'dma': dma, 'sig': False, 'deps': deps, 'final': final})
        return idx

    def barrier(self):
        self.ops.append({'eng': None, 'barrier': True, 'dma': None, 'sig': False, 'deps': set(), 'final': False})
        self.res = {}

    def emit(self):
        nc = self.nc
        ops = self.ops
        last_eng = {}
        last_dma = {}
        pend = {}
        for i, op in enumerate(ops):
            if op.get('barrier'):
                snap = set(last_eng.values()) | set(last_dma.values())
                for e in self.ENG:
                    pend[e] = set(pend.get(e, set())) | snap
                continue
            e = op['eng']
            if e in pend and pend[e]:
                for d in pend[e]:
                    op['deps'].add((d, False))
                pend[e] = set()
            if op['dma'] is not None:
                last_dma[op['dma']] = i
            else:
                last_eng[e] = i
        for op in ops:
            if op.get('barrier'):
                continue
            waits = []
            for (d, war) in op['deps']:
                p = ops[d]
                if p['dma'] is None and op['dma'] is None and p['eng'] == op['eng']:
                    if p['eng'] == 'pe' or war:
                        continue
                if p['dma'] is None:
                    p['sig'] = True
                waits.append(d)
            op['waits'] = waits
        tick = {e: 0 for e in self.ENG}
        gcount = {}
        semnames = set()
        for op in ops:
            if op.get('barrier'):
                continue
            if op['dma'] is not None:
                g = op['dma']
                gcount[g] = gcount.get(g, 0) + 16
                op['sigval'] = (('dma', g), gcount[g])
                semnames.add(op['sigval'][0])
            elif op['sig']:
                tick[op['eng']] += 1
                op['sigval'] = (('eng', op['eng']), tick[op['eng']])
                semnames.add(op['sigval'][0])
        with contextlib.ExitStack() as es:
            sems = {}
            for sn in sorted(semnames, key=str):
                sems[sn] = es.enter_context(nc.semaphore("s_" + "_".join(map(str, sn))))
            block = es.enter_context(nc.Block())
            engobj = {'pe': 'tensor', 'act': 'scalar', 'dve': 'vector', 'pool': 'gpsimd', 'sp': 'sync'}
            finals = {}
            for op in ops:
                if op.get('final'):
                    sn, val = op['sigval']
                    finals[sn] = max(finals.get(sn, 0), val)

            def make(ename):
                def body(e):
                    waited = {}
                    for op in ops:
                        if op.get('barrier') or op['eng'] != ename:
                            continue
                        for d in op['waits']:
                            sn, val = ops[d]['sigval']
                            if waited.get(sn, 0) >= val:
                                continue
                            waited[sn] = val
                            e.wait_ge(sems[sn], val)
                        ins = op['fn'](e)
                        if op['dma'] is not None:
                            ins.then_inc(sems[op['sigval'][0]], 16)
                        elif op['sig']:
                            ins.then_inc(sems[op['sigval'][0]], 1)
                    if ename == 'sp':
                        for sn, val in finals.items():
                            e.wait_ge(sems[sn], val)
                return body
            for ename in self.ENG:
                getattr(block, engobj[ename])(make(ename))


def _t5_bucket(rel):
    half, max_exact = 16, 8
    ret = np.where(rel > 0, half, 0)
    n = np.abs(rel)
    nf = np.maximum(n, 1).astype(np.float32)
    large = max_exact + (np.log(nf / np.float32(max_exact)) / np.float32(np.log(1024 / max_exact))
                         * np.float32(half - max_exact)).astype(np.int32)
    large = np.minimum(large, half - 1)
    return ret + np.where(n < max_exact, n, large)


def _bias_tables(rel_bias, half, mirror=False):
    sgn = -1 if mirror else 1
    i = np.arange(128)[:, None]
    c = np.arange(128)[None, :]
    A = np.empty((128, 8, 3, 512), np.float32)
    for bi, d in enumerate(DIL):
        rel_prev = 64 + i - c
        rel_cur = i - 64 - c
        for h in range(8):
            def tab(rel):
                v = rel_bias[_t5_bucket(sgn * rel * d), h]
                return np.where(np.abs(rel) <= 64, v, NEGB).astype(np.float32)
            tp, tcur = tab(rel_prev), tab(rel_cur)
            A[:, h, bi, 0:128] = tp
            A[:, h, bi, 128:256] = tcur
            first = tcur.copy()
            last = tp.copy()
            if half == 0:
                first[0:64, :] = NEGB
            else:
                last[64:128, :] = NEGB
            A[:, h, bi, 256:384] = first
            A[:, h, bi, 384:512] = last
    B = np.empty((128, 8, 640), np.float32)
    for h in range(8):
        def tabb(rel):
            v = rel_bias[_t5_bucket(sgn * rel), 8 + h]
            return np.where(np.abs(rel) <= 128, v, NEGB).astype(np.float32)
        t0, t1, t2 = tabb(128 + i - c), tabb(i - c), tabb(-128 + i - c)
        B[:, h, 0:128] = t0
        B[:, h, 128:256] = t1
        B[:, h, 256:384] = t2
        first = t2.copy()
        last = t0.copy()
        if half == 0:
            first[:, :] = NEGB
        else:
            last[:, :] = NEGB
        B[:, h, 384:512] = first
        B[:, h, 512:640] = last
    return A.reshape(128, -1), B.reshape(128, -1)


def _vecs(inp):
    v = np.zeros((128, 2 * VL), np.float32)
    for l in range(2):
        o = l * VL
        for name, col in (("norm_ffn1", V_NF1), ("norm_mix", V_NMIX), ("norm_ffn2", V_NF2), ("norm_ple", V_NPLE)):
            v[:, o + col:o + col + 8] = inp[name][l].reshape(8, 128).T
        for name, col in (("q_norm_a", V_QNA), ("k_norm_a", V_KNA), ("q_norm_b", V_QNB), ("k_norm_b", V_KNB)):
            v[:, o + col] = np.tile(inp[name][l], 2)
        for hb in range(8):
            v[:, o + V_SINK + hb] = inp["sink_b"][l, hb]
    return v


def _wqkv_dev(w):
    qa, ka, va, qb, kb, vb = (w[..., 0:512], w[..., 512:1024], w[..., 1024:1536], w[..., 1536:2048],
                              w[..., 2048:2176], w[..., 2176:2304])
    kbsw = np.concatenate([kb[..., 64:128], kb[..., 0:64]], axis=-1)
    return np.ascontiguousarray(np.concatenate([qa, ka, qb, kb, kbsw, va, vb], axis=-1))


def build(stages, first, last, fused=False):
    nc = bass.Bass("TRN2", target_bir_lowering=False)
    P = Prog(nc)
    if fused:
        stages = [('A', 0), ('B', 0), ('A', 1), ('B', 1)]
    layers_A = [l for (s, l) in stages if s == 'A']
    layers_B = [l for (s, l) in stages if s in ('B', 'Batt')]
    need_w = sorted(set(layers_A + layers_B))

    def din(name, shape, dt=F32):
        return nc.dram_tensor(name, shape, dt, kind="ExternalInput").ap()

    def dout(name, shape, dt=F32):
        return nc.dram_tensor(name, shape, dt, kind="ExternalOutput").ap()

    xT_in = din("xT_in", [D, T])
    if fused:
        xT_in1 = din("xT_in1", [D, T])
    vecs_d = din("vecs", [128, 2 * VL])
    ident_d = din("ident", [128, 128])
    W = {}
    for l in need_w:
        if l in layers_A:
            W[("w_in1", l)] = din(f"w_in1_{l}", [D, 2 * DFF])
            W[("w_out1", l)] = din(f"w_out1_{l}", [DFF, D])
            W[("w_qkv", l)] = din(f"w_qkv_{l}", [D, QKVW])
        if l in layers_B:
            W[("w_o", l)] = din(f"w_o_{l}", [D, D])
            W[("w_in2", l)] = din(f"w_in2_{l}", [D, 2 * DFF])
            W[("w_out2", l)] = din(f"w_out2_{l}", [DFF, D])
            W[("w_g", l)] = din(f"w_g_{l}", [D, D])
            W[("w_p", l)] = din(f"w_p_{l}", [256, D])
            W[("pT", l)] = din(f"pT_{l}", [256, T])
            if fused and l == 0:
                W[("pT1", l)] = din(f"pT1_{l}", [256, T])
    if layers_B:
        biasA_d = din("biasA", [128, 8 * 3 * 512])
        biasB_d = din("biasB", [128, 8 * 640])
        if fused:
            biasA1_d = din("biasA1", [128, 8 * 3 * 512])
            biasB1_d = din("biasB1", [128, 8 * 640])
    startB = stages[0][0] in ('B', 'Batt') and not fused
    dbg_att = stages[-1][0] == 'Batt'
    if startB:
        q_in = din("q_in", [D, T], BF16)
        kf_in = din("kf_in", [768, FR], BF16)
        vf_in = din("vf_in", [FR, 640], BF16)
    endA = stages[-1][0] == 'A'
    if fused:
        def dscr(name, shape, dt):
            return nc.dram_tensor(name, shape, dt).ap()
        xs0 = dscr("xs0", [D, T], F32)
        qs0 = dscr("qs0", [D, T], BF16)
        ks = {(l, t): dscr(f"ks_{l}_{t}", [768, FR], BF16) for l in range(2) for t in range(2)}
        vs = {(l, t): dscr(f"vs_{l}_{t}", [FR, 640], BF16) for l in range(2) for t in range(2)}
        kmid = {k_: v_[:, HALO:HALO + T] for k_, v_ in ks.items()}
        vmid = {k_: v_[HALO:HALO + T, :] for k_, v_ in vs.items()}
        kfr, vfr = ks, vs
    if endA or dbg_att:
        q_out = dout("q_out", [D, T], BF16)
        k_out = dout("k_out", [768, T], BF16)
        v_out = dout("v_out", [T, 640], BF16)
    xT_out = dout("xT_out", [D, T])

    with contextlib.ExitStack() as es:
        def sb(name, shape, dt):
            return es.enter_context(nc.sbuf_tensor(name, shape, dt))

        xT = sb("xT", [128, 8, T], F32)
        qS = sb("qS", [128, 8, T], BF16)
        vecs = sb("vecs_s", [128, 2 * VL], F32)
        vecs8 = sb("vecs8", [128, 2 * VL], F32)
        ident = sb("ident_s", [128, 128], BF16)
        ones = sb("ones", [128, 128], BF16)
        bones = sb("bones", [128, 128], BF16)
        epsb = sb("epsb", [128, 1], F32)
        ARENA = 108000
        arena = sb("arena", [128, ARENA // 2], BF16)
        ps = [es.enter_context(nc.psum_tensor(f"ps{k}", [128, 512], F32)) for k in range(8)]

        class Carver:
            def __init__(self):
                self.off = 0

            def take(self, shape, dt):
                n = int(np.prod(shape[1:]))
                nb = n * (4 if dt == F32 else 2)
                nb = (nb + 63) // 64 * 64
                assert self.off + nb <= ARENA, (self.off, nb)
                v = arena[:, self.off // 2:(self.off + nb) // 2]
                self.off += nb
                if dt == F32:
                    v = v.bitcast(F32)
                v = v[:, 0:n]
                if len(shape) == 3:
                    v = v.rearrange("p (a b) -> p a b", a=shape[1])
                elif len(shape) == 4:
                    v = v.rearrange("p (a b c) -> p a b c", a=shape[1], b=shape[2])
                return v

        P.op('dve', lambda e: e.memset(ones[:], 1.0), writes=['ones'])
        P.op('dve', lambda e: e.memset(epsb[:], EPS), writes=['epsb'])
        P.op('dve', lambda e: e.memset(bones[:], 0.0), writes=['bones'])
        P.op('dve', lambda e: e.memset(bones[0:64, 0:64], 1.0), reads=['bones'], writes=['bones'])
        P.op('dve', lambda e: e.memset(bones[64:128, 64:128], 1.0), reads=['bones'], writes=['bones'])
        P.op('pool', lambda e: e.dma_start(out=ident[:], in_=ident_d), writes=['ident'], dma='g_ident')
        P.op('sp', lambda e: e.dma_start(out=vecs[:], in_=vecs_d), writes=['vecs'], dma='g_vecs')
        P.op('dve', lambda e: e.tensor_scalar_mul(out=vecs8[:], in0=vecs[:], scalar1=0.125),
             reads=['vecs'], writes=['vecs8'])
        for l in range(2):
            c0 = l * VL + V_SINK
            P.op('act', lambda e, c0=c0: e.activation(out=vecs8[:, c0:c0 + 8], in_=vecs[:, c0:c0 + 8], func=AF.Exp),
                 reads=['vecs', 'vecs8'], writes=['vecs8'])
        def load_x(src):
            P.op('sp', lambda e: e.dma_start(out=xT[:, :, :], in_=src.rearrange("(dc p) t -> p dc t", p=128)),
                 writes=[('xT', dc, tg) for dc in range(8) for tg in range(NTG)], dma='g_xin')

        def load_q(src):
            P.op('sp', lambda e: e.dma_start(out=qS[:, :, :], in_=src.rearrange("(c p) t -> p c t", p=128)),
                 writes=[('q', c, h) for c in range(8) for h in range(2)], dma='g_qin')

        def save_x(dst, final=False):
            P.op('sp', lambda e: e.dma_start(out=dst.rearrange("(dc p) t -> p dc t", p=128), in_=xT[:, :, :]),
                 reads=[('xT', dc, tg) for dc in range(8) for tg in range(NTG)], dma='g_xout', final=final)

        def save_q(dst, final=False):
            P.op('sp', lambda e: e.dma_start(out=dst.rearrange("(c p) t -> p c t", p=128), in_=qS[:, :, :]),
                 reads=[('q', c, h) for c in range(8) for h in range(2)], dma='g_qout', final=final)

        def make_frames(l, t):
            kd, vd = ks[(l, t)], vs[(l, t)]
            if t == 0:
                kl, vl = kmid[(l, 0)][:, 0:HALO], vmid[(l, 0)][0:HALO, :]
            else:
                kl, vl = kmid[(l, 0)][:, T - HALO:T], vmid[(l, 0)][T - HALO:T, :]
            kr, vr = kmid[(l, 1)][:, 0:HALO], vmid[(l, 1)][0:HALO, :]
            for (dst, src) in ((kd[:, 0:HALO], kl), (kd[:, HALO + T:FR], kr), (vd[0:HALO, :], vl), (vd[HALO + T:FR, :], vr)):
                P.op('sp', lambda e, dst=dst, src=src: e.dma_start(out=dst, in_=src), dma='g_frames')

        if not fused:
            load_x(xT_in)
            if startB:
                load_q(q_in)

        def norm_phase(cv, hn, gcol, tgs=tuple(range(NTG))):
            sq = [cv.take([128, 512], BF16) for _ in range(3)]
            std = [cv.take([128, 512], F32) for _ in range(2)]
            k = 0
            for tg in tgs:
                ts = slice(tg * 512, (tg + 1) * 512)
                for dc in range(8):
                    s = k % 3
                    k += 1
                    P.op('act', lambda e, s=s, dc=dc, ts=ts: e.activation(out=sq[s], in_=xT[:, dc, ts], func=AF.Square,
                                                                          scale=1.0 / 32.0),
                         reads=[('xT', dc, tg)], writes=[('nsq', s)])
                    P.op('pe', lambda e, s=s, dc=dc: e.matmul(ps[7][:], ones[:], sq[s], start=(dc == 0), stop=(dc == 7)),
                         reads=[('nsq', s), 'ones'], writes=[('ps', 7)])
                b = tg % 2
                P.op('act', lambda e, b=b: e.activation(out=std[b], in_=ps[7][:], func=AF.Ln, bias=epsb[:, 0:1], scale=1.0),
                     reads=[('ps', 7), 'epsb'], writes=[('nstd', b)])
                P.op('act', lambda e, b=b: e.activation(out=std[b], in_=std[b], func=AF.Exp, scale=-0.5),
                     reads=[('nstd', b)], writes=[('nstd', b)])
                for dc in range(8):
                    P.op('dve', lambda e, b=b, dc=dc, ts=ts: e.scalar_tensor_tensor(
                        out=hn[:, dc, ts], in0=xT[:, dc, ts], scalar=vecs[:, gcol + dc:gcol + dc + 1], in1=std[b],
                        op0=ALU.mult, op1=ALU.mult),
                        reads=[('xT', dc, tg), ('nstd', b), 'vecs'], writes=[('hn', dc, tg)])

        def ffn_phase(l, w_in, w_out, gcol, tgs=tuple(range(NTG))):
            P.barrier()
            cv = Carver()
            hn = cv.take([128, 8, T], BF16)
            win = [cv.take([128, 8, 1024], BF16) for _ in range(2)]
            wout = [cv.take([128, 4, 1024], BF16) for _ in range(2)]
            act = [cv.take([128, 4, 512], BF16) for _ in range(2)]
            sg = [cv.take([128, 512], F32) for _ in range(2)]
            norm_phase(cv, hn, gcol, tgs)
            parts = [(f0, min(4, NFC - f0)) for f0 in range(0, NFC, 4)]
            w_in_v = w_in.rearrange("(dc p) f -> p dc f", p=128)
            w_out_v = w_out.rearrange("(fc p) d -> p fc d", p=128)

            def load_part(pi):
                f0, nf = parts[pi]
                s = pi % 2
                P.op('pool', lambda e: e.dma_start(out=win[s][:, :, 0:nf * 128], in_=w_in_v[:, :, f0 * 128:(f0 + nf) * 128]),
                     writes=[('wing', s)], dma=f'g_wing{s}')
                P.op('pool', lambda e: e.dma_start(out=win[s][:, :, 512:512 + nf * 128],
                                                   in_=w_in_v[:, :, DFF + f0 * 128:DFF + (f0 + nf) * 128]),
                     writes=[('winu', s)], dma=f'g_winu{s}')
                P.op('pool', lambda e: e.dma_start(out=wout[s][:, 0:nf, :], in_=w_out_v[:, f0:f0 + nf, :]),
                     writes=[('wout', s)], dma=f'g_wout{s}')

            jobs = [(pi, tg) for pi in range(len(parts)) for tg in tgs]
            gk = [0]
            yk = [0]

            def GU(ji):
                pi, tg = jobs[ji]
                f0, nf = parts[pi]
                s = pi % 2
                a = ji % 2
                ts = slice(tg * 512, (tg + 1) * 512)
                for fc in range(nf):
                    b = gk[0] % 2
                    gk[0] += 1
                    for dc in range(8):
                        P.op('pe', lambda e, dc=dc, fc=fc, b=b: e.matmul(ps[b][:], win[s][:, dc, fc * 128:(fc + 1) * 128],
                                                                         hn[:, dc, ts], start=(dc == 0), stop=(dc == 7)),
                             reads=[('wing', s), ('hn', dc, tg)], writes=[('ps', b)])
                    for dc in range(8):
                        P.op('pe', lambda e, dc=dc, fc=fc, b=b: e.matmul(ps[2 + b][:], win[s][:, dc, 512 + fc * 128:512 + (fc + 1) * 128],
                                                                         hn[:, dc, ts], start=(dc == 0), stop=(dc == 7)),
                             reads=[('winu', s), ('hn', dc, tg)], writes=[('ps', 2 + b)])
                    P.op('act', lambda e, b=b: e.activation(out=sg[b], in_=ps[b][:], func=AF.Silu),
                         reads=[('ps', b)], writes=[('sg', b)])
                    P.op('dve', lambda e, b=b, fc=fc: e.tensor_tensor(out=act[a][:, fc, :], in0=ps[2 + b][:], in1=sg[b], op=ALU.mult),
                         reads=[('ps', 2 + b), ('sg', b)], writes=[('act', a, fc)])

            def Y(ji):
                pi, tg = jobs[ji]
                f0, nf = parts[pi]
                s = pi % 2
                a = ji % 2
                ts = slice(tg * 512, (tg + 1) * 512)
                for dc in range(8):
                    b = 4 + yk[0] % 3
                    yk[0] += 1
                    for fc in range(nf):
                        P.op('pe', lambda e, dc=dc, fc=fc, b=b: e.matmul(ps[b][:], wout[s][:, fc, dc * 128:(dc + 1) * 128],
                                                                         act[a][:, fc, :], start=(fc == 0), stop=(fc == nf - 1)),
                             reads=[('wout', s), ('act', a, fc)], writes=[('ps', b)])
                    P.op('dve', lambda e, dc=dc, b=b: e.scalar_tensor_tensor(out=xT[:, dc, ts], in0=ps[b][:], scalar=0.5,
                                                                            in1=xT[:, dc, ts], op0=ALU.mult, op1=ALU.add),
                         reads=[('ps', b), ('xT', dc, tg)], writes=[('xT', dc, tg)])

            load_part(0)
            load_part(1)
            for ji in range(len(jobs) + 1):
                if ji < len(jobs):
                    pi, tg = jobs[ji]
                    GU(ji)
                if ji >= 1:
                    Y(ji - 1)
                    ppi, ptg = jobs[ji - 1]
                    if ptg == tgs[-1] and ppi + 2 < len(parts):
                        load_part(ppi + 2)

        def qkv_phase(l, k_out, v_out, final=True, tgs=tuple(range(NTG))):
            P.barrier()
            cv = Carver()
            hn = cv.take([128, 8, T], BF16)
            wq = cv.take([128, 8, QKVW], BF16)
            kst = [cv.take([128, T], BF16) for _ in range(2)]
            vst = [cv.take([128, 640], BF16) for _ in range(2)]
            sq = [cv.take([128, 512], BF16) for _ in range(4)]
            std = [cv.take([128, 512], F32) for _ in range(4)]
            wv = W[("w_qkv", l)].rearrange("(dc p) f -> p dc f", p=128)
            for (c0, c1) in ((0, 1024), (1024, 1792), (1792, QKVW)):
                P.op('pool', lambda e, c0=c0, c1=c1: e.dma_start(out=wq[:, :, c0:c1], in_=wv[:, :, c0:c1]),
                     writes=[('wq', c0)], dma=f'g_wq{c0}')
            norm_phase(cv, hn, l * VL + V_NMIX, tgs)
            vo = l * VL
            units = [(ch, tg) for ch in range(14) for tg in tgs]
            ncol = 512 * len(tgs)

            def unit_info(ch):
                if ch < 4:
                    return 'q', vecs8[:, vo + V_QNA:vo + V_QNA + 1], ch
                elif ch < 8:
                    return 'k', vecs[:, vo + V_KNA:vo + V_KNA + 1], ch - 4
                elif ch < 12:
                    return 'q', vecs8[:, vo + V_QNB:vo + V_QNB + 1], ch - 4
                return 'k', vecs[:, vo + V_KNB:vo + V_KNB + 1], ch - 8

            def proj(ui):
                ch, tg = units[ui]
                b = ui % 4
                ts = slice(tg * 512, (tg + 1) * 512)
                wres = ('wq', 0) if ch < 8 else ('wq', 1024)
                for dc in range(8):
                    P.op('pe', lambda e, dc=dc: e.matmul(ps[b][:], wq[:, dc, ch * 128:(ch + 1) * 128], hn[:, dc, ts],
                                                         start=(dc == 0), stop=(dc == 7)),
                         reads=[wres, ('hn', dc, tg)], writes=[('ps', b)])
                P.op('act', lambda e: e.activation(out=sq[b], in_=ps[b][:], func=AF.Square, scale=0.125),
                     reads=[('ps', b)], writes=[('qsq', b)])

            def stats(ui):
                ch, tg = units[ui]
                b = ui % 4
                sb_ = 4 + ui % 2
                ts = slice(tg * 512, (tg + 1) * 512)
                kind, gv, dst = unit_info(ch)
                P.op('pe', lambda e: e.matmul(ps[sb_][:], bones[:], sq[b], start=True, stop=True),
                     reads=[('qsq', b), 'bones'], writes=[('ps', sb_)])
                P.op('act', lambda e: e.activation(out=std[b], in_=ps[sb_][:], func=AF.Ln, bias=epsb[:, 0:1], scale=1.0),
                     reads=[('ps', sb_), 'epsb'], writes=[('qstd', b)])
                P.op('act', lambda e: e.activation(out=std[b], in_=std[b], func=AF.Exp, scale=-0.5),
                     reads=[('qstd', b)], writes=[('qstd', b)])
                if kind == 'q':
                    P.op('dve', lambda e: e.scalar_tensor_tensor(
                        out=qS[:, dst, ts], in0=ps[b][:], scalar=gv, in1=std[b], op0=ALU.mult, op1=ALU.mult),
                        reads=[('ps', b), ('qstd', b), 'vecs', 'vecs8'], writes=[('q', dst, 0), ('q', dst, 1)])
                else:
                    ksl = dst % 2
                    P.op('dve', lambda e: e.scalar_tensor_tensor(
                        out=kst[ksl][:, ts], in0=ps[b][:], scalar=gv, in1=std[b], op0=ALU.mult, op1=ALU.mult),
                        reads=[('ps', b), ('qstd', b), 'vecs'], writes=[('kst', ksl)])
                    if tg == tgs[-1]:
                        P.op('sp', lambda e: e.dma_start(out=k_out[dst * 128:(dst + 1) * 128, 0:ncol], in_=kst[ksl][:, 0:ncol]),
                             reads=[('kst', ksl)], dma=f'g_kout{ksl}', final=final)

            LAQ = 3
            for ui in range(len(units) + LAQ):
                if ui < len(units):
                    proj(ui)
                if ui >= LAQ:
                    stats(ui - LAQ)
            for tt in range(4 * len(tgs)):
                tsl = slice(tt * 128, (tt + 1) * 128)
                b = tt % 2
                for dc in range(8):
                    P.op('pe', lambda e, dc=dc, b=b, tsl=tsl: e.matmul(ps[4 + b][:], hn[:, dc, tsl], wq[:, dc, 1792:2304],
                                                                       start=(dc == 0), stop=(dc == 7)),
                         reads=[('wq', 1792), ('hn', dc, tt // 4)], writes=[('ps', 4 + b)])
                for dc in range(8):
                    P.op('pe', lambda e, dc=dc, b=b, tsl=tsl: e.matmul(ps[6 + b][:, 0:128], hn[:, dc, tsl], wq[:, dc, 2304:2432],
                                                                       start=(dc == 0), stop=(dc == 7)),
                         reads=[('wq', 1792), ('hn', dc, tt // 4)], writes=[('ps', 6 + b)])
                P.op('act', lambda e, b=b: e.copy(out=vst[b][:, 0:512], in_=ps[4 + b][:]),
                     reads=[('ps', 4 + b)], writes=[('vst', b)])
                P.op('dve', lambda e, b=b: e.tensor_copy(out=vst[b][:, 512:640], in_=ps[6 + b][:, 0:128]),
                     reads=[('ps', 6 + b)], writes=[('vstb', b)])
                P.op('sp', lambda e, b=b, tsl=tsl: e.dma_start(out=v_out[tsl, :], in_=vst[b]),
                     reads=[('vst', b), ('vstb', b)], dma=f'g_vout{b}', final=final)

        def att_phase(l, kf_d, vf_d, biasA_d, biasB_d, qhalf=False):
            P.barrier()
            cv = Carver()
            kfb = [cv.take([128, FR], BF16) for _ in range(2)]
            NTA = 17 + 20 + 32
            VtA = cv.take([128, NTA, 2, 128], BF16)
            VtB = cv.take([128, 18, 2, 128], BF16)
            biasb = [cv.take([128, 3072], BF16) for _ in range(2)]
            PT = {bk: cv.take([128, 512], BF16) for bk in (0, 1, 2, 5, 6, 7)}
            SBANKS = (0, 1, 2, 5, 6, 7)
            Uacc = [cv.take([128, T], F32) for _ in range(2)]
            rz = cv.take([128, T], F32)
            vo = l * VL
            P.op('dve', lambda e: e.memset(VtA[:, :, :, 64:128], 1.0), writes=['Vones'])
            P.op('dve', lambda e: e.memset(VtB[:, :, :, 64:128], 1.0), reads=['Vones'], writes=['Vones'])
            tbase = {}
            n = 0
            for bi, d in enumerate(DIL):
                nqb = 16 // d
                for r in range(d):
                    tbase[(bi, r)] = n
                    n += nqb + 1
            assert n == NTA
            sk = [0]
            ok = [0]

            stream = []

            def run_head(ch, hp, h, jobs, Ua, vt, ku, bs, pre=(), post=()):
                for job in jobs:
                    job.update(ch=ch, hp=hp, Ua=Ua, vtens=vt, ku=ku, bs=bs, pre=[], post=[])
                jobs[0]['pre'] = list(pre)
                jobs[-1]['post'] = list(post)
                stream.extend(jobs)

            def S(job):
                b = SBANKS[sk[0] % 6]
                sk[0] += 1
                job['sb'] = b
                N = job['N']
                ch, hp, ku, bs = job['ch'], job['hp'], job['ku'], job['bs']
                bias = biasb[bs]
                P.op('pe', lambda e: e.matmul(ps[b][:, 0:N], job['k'], job['q'], start=True, stop=True),
                     reads=[('kf', ku), ('q', ch, hp)], writes=[('ps', b)])
                P.op('act', lambda e: e.activation(out=PT[b][:, 0:N], in_=ps[b][:, 0:N], func=AF.Exp),
                     reads=[('ps', b)], writes=[('PT', b)])
                P.op('dve', lambda e: e.tensor_tensor(out=PT[b][:, 0:N], in0=PT[b][:, 0:N], in1=bias[:, job['bc']:job['bc'] + N], op=ALU.mult),
                     reads=[('PT', b), ('bias', bs)], writes=[('PT', b)])

            def PV(job):
                b = job['sb']
                hp, vt = job['hp'], job['vtens']
                blks = job['blocks']
                (col0, grp, slot0, st, _sp, _f) = blks[0]
                nb = len(blks)
                for k_, bl in enumerate(blks):
                    assert bl[2] == slot0 + k_ and bl[0] == col0 + 128 * k_
                ob = 3 + grp % 2
                fin = blks[-1][5]
                P.op('pe', lambda e: e.matmul(
                    ps[ob][:, slot0 * 128:(slot0 + nb) * 128], vt[:, job['vt'], job['vh'], :],
                    PT[b][:, col0:col0 + 128 * nb], start=st, stop=True, skip_group_check=True),
                    reads=[('PT', b), job['vres'], 'Vones'], writes=[('ps', ob)])
                if fin is not None:
                    dst, src_ap, first = fin
                    if first:
                        P.op('dve', lambda e: e.tensor_copy(out=dst, in_=src_ap(ps[ob])),
                             reads=[('ps', ob)], writes=[('Ua', hp)])
                    else:
                        P.op('dve', lambda e: e.tensor_tensor(out=dst, in0=src_ap(ps[ob]), in1=dst, op=ALU.add),
                             reads=[('ps', ob), ('Ua', hp)], writes=[('Ua', hp)])

            def flush_stream():
                LA = 5
                DEFER = 6
                pending = []
                for i in range(len(stream) + LA):
                    if i < len(stream):
                        for f in stream[i]['pre']:
                            f()
                        S(stream[i])
                    if i >= LA:
                        job = stream[i - LA]
                        PV(job)
                        for f in job['post']:
                            f()
                        for k_, f in enumerate(job.get('fin', ())):
                            pending.append((i + DEFER + k_, f))
                    if pending and pending[0][0] <= i:
                        pending.pop(0)[1]()
                for (_r, f) in pending:
                    f()

            gcount = [0]

            def make_jobs_A(ch, hp, Ua, kf):
                jobs = []
                qv = qS[hp * 64:(hp + 1) * 64, ch, :]
                kv = kf[hp * 64:(hp + 1) * 64, :]
                bh = hp * 1536
                for bi, d in enumerate(DIL):
                    nqb = 16 // d
                    if d == 16:
                        groups = [[(r, 0) for r in range(g * 4, g * 4 + 4)] for g in range(4)]
                    elif d == 4:
                        groups = [[(r, b) for b in range(2 if qhalf else 4)] for r in range(4)]
                    else:
                        groups = [[(0, b) for b in range(g * 4, g * 4 + 4)] for g in range(2 if qhalf else 4)]
                    for grp_blocks in groups:
                        grp = gcount[0]
                        gcount[0] += 1
                        slot_of = {rb: s for s, rb in enumerate(grp_blocks)}
                        classes = sorted(set(r for r, _ in grp_blocks))
                        if d == 1:
                            b0 = grp_blocks[0][1]
                            dst = Ua[:, b0 * 128:b0 * 128 + 512]
                            src_ap = (lambda p: p[:, 0:512])
                        elif d == 4:
                            r = grp_blocks[0][0]
                            nm = 128 * len(grp_blocks)
                            dst = Ua[:, :].rearrange("p (m r) -> p r m", r=4)[:, r, 0:nm]
                            src_ap = (lambda p, nm=nm: p[:, 0:nm])
                        else:
                            r0 = grp_blocks[0][0]
                            dst = Ua[:, :].rearrange("p (m r) -> p r m", r=16)[:, r0:r0 + 4, :]
                            src_ap = (lambda p: p[:, 0:512].rearrange("p (r m) -> p r m", r=4))
                        first = (bi == 0)
                        pending = []
                        for r in classes:
                            blks = sorted(b for (rr, b) in grp_blocks if rr == r)
                            tiles = sorted(set([b for b in blks] + [b + 1 for b in blks]))
                            for j in tiles:
                                served = [b for b in (j - 1, j) if b in blks]
                                bq0 = served[0]
                                N = 128 * len(served)
                                kstart = HALO + d * (128 * j - 64) + r
                                kap = kv.rearrange("p (m r) -> p r m", r=d)[:, kstart % d, kstart // d:kstart // d + 128]
                                qstart = d * 128 * bq0 + r
                                qap = qv.rearrange("p (m r) -> p r m", r=d)[:, qstart % d, qstart // d:qstart // d + N]
                                if j == 0:
                                    bc = bh + bi * 512 + 256
                                elif j == nqb:
                                    bc = bh + bi * 512 + 384
                                else:
                                    bc = bh + bi * 512 + (0 if served[0] == j - 1 else 128)
                                blocks = []
                                for si, b in enumerate(served):
                                    st = (len(pending) == 0)
                                    pending.append(1)
                                    sp_ = (j == b + 1)
                                    blocks.append([si * 128, grp, slot_of[(r, b)], st, sp_, None])
                                jobs.append({'k': kap, 'q': qap, 'N': N, 'bc': bc, 'vt': tbase[(bi, r)] + j, 'vh': hp,
                                             'blocks': blocks, 'vres': ('Vt', bi, hp)})
                        jobs[-1]['blocks'][-1][5] = (dst, src_ap, first)
                return jobs

            def make_jobs_B(ch, hp, Ua, kvh, kf):
                jobs = []
                qv = qS[hp * 64:(hp + 1) * 64, ch, :]
                kv = kf[hp * 64:(hp + 1) * 64, :]
                bh = hp * 640
                for g in range(2 if qhalf else 4):
                    grp = gcount[0]
                    gcount[0] += 1
                    blks = list(range(g * 4, g * 4 + 4))
                    tiles = list(range(blks[0], blks[-1] + 3))
                    for j in tiles:
                        served = [b for b in (j - 2, j - 1, j) if b in blks]
                        N = 128 * len(served)
                        kstart = HALO + 128 * (j - 1)
                        kap = kv[:, kstart:kstart + 128]
                        qap = qv[:, served[0] * 128:served[0] * 128 + N]
                        if j == 0:
                            bc = bh + 384
                        elif j == 17:
                            bc = bh + 512
                        else:
                            bc = bh + 128 * (served[0] - (j - 2))
                        blocks = []
                        for si, b in enumerate(served):
                            blocks.append([si * 128, grp, b - blks[0], (j == tiles[0] and si == 0), (j == b + 2), None])
                        jobs.append({'k': kap, 'q': qap, 'N': N, 'bc': bc, 'vt': j, 'vh': kvh, 'blocks': blocks, 'vres': ('VtB', kvh)})
                    dst = Ua[:, blks[0] * 128:blks[0] * 128 + 512]
                    jobs[-1]['blocks'][-1][5] = (dst, (lambda p: p[:, 0:512]), True)
                return jobs

            def finish_head(ch, hp, Ua, sinkcol):
                hs = slice(hp * 64, (hp + 1) * 64)
                nq = T // 2 if qhalf else T
                pieces = []
                if sinkcol is not None:
                    pieces.append(lambda: P.op('act', lambda e: e.activation(out=rz[64:128, 0:nq], in_=Ua[64:128, 0:nq], func=AF.Ln,
                                                                             bias=vecs8[64:128, sinkcol:sinkcol + 1], scale=1.0),
                                               reads=[('Ua', hp), 'vecs8'], writes=[('rz',)]))
                else:
                    pieces.append(lambda: P.op('act', lambda e: e.activation(out=rz[64:128, 0:nq], in_=Ua[64:128, 0:nq], func=AF.Ln),
                                               reads=[('Ua', hp)], writes=[('rz',)]))
                pieces.append(lambda: P.op('act', lambda e: e.activation(out=rz[64:128, 0:nq], in_=rz[64:128, 0:nq], func=AF.Exp, scale=-1.0),
                                           reads=[('rz',)], writes=[('rz',)]))
                nchunk = 4
                w = nq // nchunk
                for k_ in range(nchunk):
                    cs = slice(k_ * w, (k_ + 1) * w)
                    pieces.append(lambda cs=cs, k_=k_: P.op('dve', lambda e: e.tensor_copy(out=rz[0:64, cs], in_=rz[64:128, cs]),
                                                            reads=[('rz',)], writes=[('rzs', k_)]))
                    pieces.append(lambda cs=cs, k_=k_: P.op('dve', lambda e: e.tensor_tensor(out=qS[hs, ch, cs], in0=Ua[0:64, cs], in1=rz[0:64, cs], op=ALU.mult),
                                                            reads=[('Ua', hp), ('rzs', k_)], writes=[('q', ch, hp)]))
                return pieces

            units = [('A', ch, None) for ch in range(4)] + [('B', ch, hp) for ch in range(4, 8) for hp in range(2)]

            def kf_bias_dma(u):
                kind, ch, hp = units[u]
                ku = u % 2
                if kind == 'A':
                    P.op('sp', lambda e: e.dma_start(out=kfb[ku][:, :], in_=kf_d[ch * 128:(ch + 1) * 128, :]),
                         writes=[('kf', ku)], dma=f'g_kf{ku}')
                    bs = ch % 2
                    P.op('pool', lambda e: e.dma_start(out=biasb[bs][:, 0:3072], in_=biasA_d[:, ch * 3072:(ch + 1) * 3072]),
                         writes=[('biasraw', bs), ('bias', bs)], dma=f'g_bias{bs}')
                else:
                    hb = (ch - 4) * 2 + hp
                    kvh = hb // 4
                    krow = 512 if kvh == hp else 640
                    P.op('sp', lambda e: e.dma_start(out=kfb[ku][:, :], in_=kf_d[krow:krow + 128, :]),
                         writes=[('kf', ku)], dma=f'g_kf{ku}')
                    if hp == 0:
                        bs = ch % 2
                        P.op('pool', lambda e: e.dma_start(out=biasb[bs][:, 0:1280], in_=biasB_d[:, (ch - 4) * 1280:(ch - 4 + 1) * 1280]),
                             writes=[('biasraw', bs), ('bias', bs)], dma=f'g_bias{bs}')

            def bias_exp(u):
                kind, ch, hp = units[u]
                if kind == 'B' and hp == 1:
                    return
                bs = ch % 2
                n = 3072 if kind == 'A' else 1280
                P.op('act', lambda e: e.activation(out=biasb[bs][:, 0:n], in_=biasb[bs][:, 0:n], func=AF.Exp),
                     reads=[('biasraw', bs)], writes=[('bias', bs)])

            def v_loads(ch):
                for hh in range(2):
                    for bi, d in enumerate(DIL):
                        nqb = 16 // d
                        for r in range(d):
                            for tlo, thi in ([(0, 9), (9, 17)] if d == 1 else [(0, nqb + 1)]):
                                nt = thi - tlo
                                row0 = HALO + d * (128 * tlo - 64) + r
                                t0 = tbase[(bi, r)] + tlo
                                src = bass.AP(vf_d.tensor, row0 * 640 + ch * 128 + hh * 64,
                                              [[d * 640, 128], [d * 128 * 640, nt], [1, 64]])
                                P.op('sp', lambda e, src=src, t0=t0, nt=nt, hh=hh: e.dma_start(out=VtA[:, t0:t0 + nt, hh, 0:64], in_=src),
                                     also_writes=[('Vt', bi, hh)], dma=f'g_vt{bi}{hh}')

            for hh in range(2):
                src = bass.AP(vf_d.tensor, (HALO - 128) * 640 + 512 + hh * 64, [[640, 128], [128 * 640, 18], [1, 64]])
                P.op('sp', lambda e, src=src, hh=hh: e.dma_start(out=VtB[:, :, hh, 0:64], in_=src), writes=[('VtB', hh)], dma=f'g_vtb{hh}')
            kf_bias_dma(0)
            v_loads(0)
            for u, (kind, ch, hp_) in enumerate(units):
                pre = []
                if u + 1 < len(units):
                    pre.append(lambda u=u: kf_bias_dma(u + 1))
                pre.append(lambda u=u: bias_exp(u))
                ku = u % 2
                bs = ch % 2
                if kind == 'A':
                    for hp in range(2):
                        Ua = Uacc[hp]
                        jobs = make_jobs_A(ch, hp, Ua, kfb[ku])
                        post = []
                        if hp == 1 and ch + 1 < 4:
                            post.append(lambda ch=ch: v_loads(ch + 1))
                        run_head(ch, hp, None, jobs, Ua, VtA, ku, bs, pre=(pre if hp == 0 else ()), post=post)
                        jobs[-1]['fin'] = finish_head(ch, hp, Ua, None)
                else:
                    hp = hp_
                    Ua = Uacc[hp]
                    hb = (ch - 4) * 2 + hp
                    kvh = hb // 4
                    jobs = make_jobs_B(ch, hp, Ua, kvh, kfb[ku])
                    run_head(ch, hp, None, jobs, Ua, VtB, ku, bs, pre=pre, post=[])
                    jobs[-1]['fin'] = finish_head(ch, hp, Ua, vo + V_SINK + hb)
            flush_stream()

        def wo_phase(l, tgs=tuple(range(NTG))):
            P.barrier()
            cv = Carver()
            wo = cv.take([128, 8, D], BF16)
            wv = W[("w_o", l)].rearrange("(c p) d -> p c d", p=128)
            for h in range(2):
                P.op('pool', lambda e, h=h: e.dma_start(out=wo[:, 4 * h:4 * h + 4, :], in_=wv[:, 4 * h:4 * h + 4, :]),
                     writes=[('wo', h)], dma=f'g_wo{h}')
            k = 0
            for tg in tgs:
                ts = slice(tg * 512, (tg + 1) * 512)
                for dc in range(8):
                    b = k % 4
                    k += 1
                    for c in range(8):
                        P.op('pe', lambda e, c=c, dc=dc, b=b, ts=ts: e.matmul(ps[b][:], wo[:, c, dc * 128:(dc + 1) * 128], qS[:, c, ts],
                                                                              start=(c == 0), stop=(c == 7)),
                             reads=[('wo', c // 4), ('q', c, 0), ('q', c, 1)], writes=[('ps', b)])
                    P.op('dve', lambda e, dc=dc, b=b, ts=ts: e.tensor_tensor(out=xT[:, dc, ts], in0=ps[b][:], in1=xT[:, dc, ts], op=ALU.add),
                         reads=[('ps', b), ('xT', dc, tg)], writes=[('xT', dc, tg)])

        def ple_phase(l, pT_d, tgs=tuple(range(NTG))):
            P.barrier()
            cv = Carver()
            hn = cv.take([128, 8, T], BF16)
            wg = cv.take([128, 8, D], BF16)
            wp = cv.take([128, 2, D], BF16)
            pT = cv.take([128, 2, T], BF16)
            sgm = [cv.take([128, 512], F32) for _ in range(2)]
            wgv = W[("w_g", l)].rearrange("(c p) d -> p c d", p=128)
            wpv = W[("w_p", l)].rearrange("(c p) d -> p c d", p=128)
            ptv = pT_d.rearrange("(c p) t -> p c t", p=128)
            for h in range(2):
                P.op('pool', lambda e, h=h: e.dma_start(out=wg[:, 4 * h:4 * h + 4, :], in_=wgv[:, 4 * h:4 * h + 4, :]),
                     writes=[('wg', h)], dma=f'g_wg{h}')
            P.op('pool', lambda e: e.dma_start(out=wp[:, :, :], in_=wpv), writes=[('wp',)], dma='g_wp')
            P.op('pool', lambda e: e.dma_start(out=pT[:, :, :], in_=ptv), writes=[('pT',)], dma='g_pT')
            norm_phase(cv, hn, l * VL + V_NPLE, tgs)
            k = 0
            for tg in tgs:
                ts = slice(tg * 512, (tg + 1) * 512)
                for dc in range(8):
                    b = k % 2
                    k += 1
                    for c in range(8):
                        P.op('pe', lambda e, c=c, dc=dc, b=b, ts=ts: e.matmul(ps[b][:], wg[:, c, dc * 128:(dc + 1) * 128], hn[:, c, ts],
                                                                              start=(c == 0), stop=(c == 7)),
                             reads=[('wg', c // 4), ('hn', c, tg)], writes=[('ps', b)])
                    for c in range(2):
                        P.op('pe', lambda e, c=c, dc=dc, b=b, ts=ts: e.matmul(ps[2 + b][:], wp[:, c, dc * 128:(dc + 1) * 128], pT[:, c, ts],
                                                                              start=(c == 0), stop=(c == 1)),
                             reads=[('wp',), ('pT',)], writes=[('ps', 2 + b)])
                    P.op('act', lambda e, b=b: e.activation(out=sgm[b], in_=ps[b][:], func=AF.Sigmoid),
                         reads=[('ps', b)], writes=[('sgm', b)])
                    P.op('dve', lambda e, b=b: e.tensor_tensor(out=sgm[b], in0=ps[2 + b][:], in1=sgm[b], op=ALU.mult),
                         reads=[('ps', 2 + b), ('sgm', b)], writes=[('sgm', b)])
                    P.op('dve', lambda e, b=b, dc=dc, ts=ts: e.tensor_tensor(out=xT[:, dc, ts], in0=sgm[b], in1=xT[:, dc, ts], op=ALU.add),
                         reads=[('sgm', b), ('xT', dc, tg)], writes=[('xT', dc, tg)])

        ALLTG = tuple(range(NTG))

        def stage_A(l, k_dst, v_dst, final, tgs=ALLTG):
            ffn_phase(l, W[("w_in1", l)], W[("w_out1", l)], l * VL + V_NF1, tgs)
            qkv_phase(l, k_dst, v_dst, final, tgs)

        def stage_B(l, kf_d, vf_d, bA, bB, pT_d, tgs=ALLTG):
            att_phase(l, kf_d, vf_d, bA, bB, qhalf=(len(tgs) < NTG))
            wo_phase(l, tgs)
            ffn_phase(l, W[("w_in2", l)], W[("w_out2", l)], l * VL + V_NF2, tgs)
            ple_phase(l, pT_d, tgs)

        if not fused:
            for (s, l) in stages:
                if s == 'A':
                    stage_A(l, k_out, v_out, True)
                elif s == 'Batt':
                    att_phase(l, kf_in, vf_in, biasA_d, biasB_d)
                else:
                    stage_B(l, kf_in, vf_in, biasA_d, biasB_d, W[("pT", l)])
            P.barrier()
            save_x(xT_out, final=True)
            if endA or dbg_att:
                save_q(q_out, final=True)
        else:
            load_x(xT_in)
            stage_A(0, kmid[(0, 0)], vmid[(0, 0)], False)
            P.barrier()
            save_x(xs0)
            save_q(qs0)
            P.barrier()
            load_x(xT_in1)
            stage_A(0, kmid[(0, 1)], vmid[(0, 1)], False)
            P.barrier()
            make_frames(0, 1)
            make_frames(0, 0)
            stage_B(0, kfr[(0, 1)], vfr[(0, 1)], biasA1_d, biasB1_d, W[("pT1", 0)], tgs=(0, 1))
            stage_A(1, kmid[(1, 1)], vmid[(1, 1)], False, tgs=(0, 1))
            P.barrier()
            load_x(xs0)
            load_q(qs0)
            stage_B(0, kfr[(0, 0)], vfr[(0, 0)], biasA_d, biasB_d, W[("pT", 0)])
            stage_A(1, kmid[(1, 0)], vmid[(1, 0)], False)
            P.barrier()
            make_frames(1, 0)
            stage_B(1, kfr[(1, 0)], vfr[(1, 0)], biasA_d, biasB_d, W[("pT", 1)])
            P.barrier()
            save_x(xT_out, final=True)
        P.emit()
    return nc


_PROGS = {}


def _get_prog(key, stages, first, last, fused=False):
    if key not in _PROGS:
        _PROGS[key] = build(stages, first, last, fused=fused)
    return _PROGS[key]


def _frames(k_out, v_out):
    kfs, vfs = [], []
    for c in range(NCORES):
        half = c % 2
        kf = np.zeros((768, FR), k_out[c].dtype)
        vf = np.zeros((FR, 640), v_out[c].dtype)
        kf[:, HALO:HALO + T] = k_out[c]
        vf[HALO:HALO + T] = v_out[c]
        if half == 0:
            kf[:, HALO + T:] = k_out[c + 1][:, 0:HALO]
            vf[HALO + T:] = v_out[c + 1][0:HALO]
        else:
            kf[:, 0:HALO] = k_out[c - 1][:, T - HALO:T]
            vf[0:HALO] = v_out[c - 1][T - HALO:T]
        kfs.append(kf)
        vfs.append(vf)
    return kfs, vfs


def kernel(**inp):
    inp = {k: np.asarray(v) for k, v in inp.items()}
    x, p = inp["x"], inp["p"]
    vecs = _vecs(inp)
    wqkv = _wqkv_dev(inp["w_qkv"])
    tabs = [[_bias_tables(inp["rel_bias"], t, mirror=bool(m)) for t in range(2)] for m in range(2)]
    identh = np.eye(128, dtype=np.float32)
    cores = list(range(NCORES))
    nc = _get_prog("fused", None, True, True, fused=True)
    maps = []
    S = 2 * T
    for c in cores:
        b, half = c // 2, c % 2
        order = np.arange(S) if half == 0 else np.arange(S - 1, -1, -1)
        own, oth = order[0:T], order[T:S]
        tb = tabs[half]
        m = dict(xT_in=np.ascontiguousarray(x[b, own, :].T), xT_in1=np.ascontiguousarray(x[b, oth, :].T),
                 vecs=vecs, ident=identh,
                 biasA=tb[0][0], biasB=tb[0][1], biasA1=tb[1][0], biasB1=tb[1][1],
                 pT_0=np.ascontiguousarray(p[0, b, own, :].T), pT1_0=np.ascontiguousarray(p[0, b, oth, :].T),
                 pT_1=np.ascontiguousarray(p[1, b, own, :].T))
        for l in range(2):
            m.update({f"w_in1_{l}": inp["ffn1_w_in"][l], f"w_out1_{l}": inp["ffn1_w_out"][l], f"w_qkv_{l}": wqkv[l],
                      f"w_o_{l}": inp["w_o"][l], f"w_in2_{l}": inp["ffn2_w_in"][l], f"w_out2_{l}": inp["ffn2_w_out"][l],
                      f"w_g_{l}": inp["w_ple_gate"][l], f"w_p_{l}": inp["w_ple_proj"][l]})
        maps.append(m)
    res = run_bass_kernel_spmd(nc, maps, core_ids=cores).results
    out = np.empty((4, S, D), np.float32)
    for c in cores:
        b, half = c // 2, c % 2
        order = np.arange(S) if half == 0 else np.arange(S - 1, -1, -1)
        out[b, order[0:T], :] = res[c]["xT_out"].T
    return out
```

```python
import contextlib
import numpy as np
import ml_dtypes
import concourse.bass as bass
import concourse.mybir as mybir
from concourse.bass_utils import run_bass_kernel_spmd

F32 = mybir.dt.float32
BF16 = mybir.dt.bfloat16
ALU = mybir.AluOpType
AF = mybir.ActivationFunctionType

D = 1024
T = 2048
NTG = 4
DFF = 2816
NFC = 22
HALO = 1024
FR = T + 2 * HALO
QKVW = 2432
NEGB = -30000.0
EPS = 1e-6
DIL = (1, 4, 16)
NCORES = 8

V_NF1, V_NMIX, V_NF2, V_NPLE = 0, 8, 16, 24
V_QNA, V_KNA, V_QNB, V_KNB = 32, 33, 34, 35
V_SINK = 36
VL = 44


class Prog:
    ENG = ('pe', 'act', 'dve', 'pool', 'sp')

    def __init__(self, nc):
        self.nc = nc
        self.ops = []
        self.res = {}
        self.pending_barrier = None

    def op(self, eng, fn, reads=(), writes=(), dma=None, final=False, also_writes=()):
        idx = len(self.ops)
        deps = set()
        for w in also_writes:
            st = self.res.setdefault(w, [None, []])
            for r in st[1]:
                if r != idx:
                    deps.add((r, True))
            st[0] = idx
            st[1] = []
        for r in reads:
            st = self.res.setdefault(r, [None, []])
            if st[0] is not None:
                deps.add((st[0], False))
            st[1].append(idx)
        for w in writes:
            st = self.res.setdefault(w, [None, []])
            if st[0] is not None:
                deps.add((st[0], False))
            for r in st[1]:
                if r != idx:
                    deps.add((r, True))
            st[0] = idx
            st[1] = []
        self.ops.append({'eng': eng, 'fn': fn, 'dma': dma, 'sig': False, 'deps': deps, 'final': final})
        return idx

    def barrier(self):
        self.ops.append({'eng': None, 'barrier': True, 'dma': None, 'sig': False, 'deps': set(), 'final': False})
        self.res = {}

    def emit(self):
        nc = self.nc
        ops = self.ops
        last_eng = {}
        last_dma = {}
        pend = {}
        for i, op in enumerate(ops):
            if op.get('barrier'):
                snap = set(last_eng.values()) | set(last_dma.values())
                for e in self.ENG:
                    pend[e] = set(pend.get(e, set())) | snap
                continue
            e = op['eng']
            if e in pend and pend[e]:
                for d in pend[e]:
                    op['deps'].add((d, False))
                pend[e] = set()
            if op['dma'] is not None:
                last_dma[op['dma']] = i
            else:
                last_eng[e] = i
        for op in ops:
            if op.get('barrier'):
                continue
            waits = []
            for (d, war) in op['deps']:
                p = ops[d]
                if p['dma'] is None and op['dma'] is None and p['eng'] == op['eng']:
                    if p['eng'] == 'pe' or war:
                        continue
                if p['dma'] is None:
                    p['sig'] = True
                waits.append(d)
            op['waits'] = waits
        tick = {e: 0 for e in self.ENG}
        gcount = {}
        semnames = set()
        for op in ops:
            if op.get('barrier'):
                continue
            if op['dma'] is not None:
                g = op['dma']
                gcount[g] = gcount.get(g, 0) + 16
                op['sigval'] = (('dma', g), gcount[g])
                semnames.add(op['sigval'][0])
            elif op['sig']:
                tick[op['eng']] += 1
                op['sigval'] = (('eng', op['eng']), tick[op['eng']])
                semnames.add(op['sigval'][0])
        with contextlib.ExitStack() as es:
            sems = {}
            for sn in sorted(semnames, key=str):
                sems[sn] = es.enter_context(nc.semaphore("s_" + "_".join(map(str, sn))))
            block = es.enter_context(nc.Block())
            engobj = {'pe': 'tensor', 'act': 'scalar', 'dve': 'vector', 'pool': 'gpsimd', 'sp': 'sync'}
            finals = {}
            for op in ops:
                if op.get('final'):
                    sn, val = op['sigval']
                    finals[sn] = max(finals.get(sn, 0), val)

            def make(ename):
                def body(e):
                    waited = {}
                    for op in ops:
                        if op.get('barrier') or op['eng'] != ename:
                            continue
                        for d in op['waits']:
                            sn, val = ops[d]['sigval']
                            if waited.get(sn, 0) >= val:
                                continue
                            waited[sn] = val
                            e.wait_ge(sems[sn], val)
                        ins = op['fn'](e)
                        if op['dma'] is not None:
                            ins.then_inc(sems[op['sigval'][0]], 16)
                        elif op['sig']:
                            ins.then_inc(sems[op['sigval'][0]], 1)
                    if ename == 'sp':
                        for sn, val in finals.items():
                            e.wait_ge(sems[sn], val)
                return body
            for ename in self.ENG:
                getattr(block, engobj[ename])(make(ename))


def _t5_bucket(rel):
    half, max_exact = 16, 8
    ret = np.where(rel > 0, half, 0)
    n = np.abs(rel)
    nf = np.maximum(n, 1).astype(np.float32)
    large = max_exact + (np.log(nf / np.float32(max_exact)) / np.float32(np.log(1024 / max_exact))
                         * np.float32(half - max_exact)).astype(np.int32)
    large = np.minimum(large, half - 1)
    return ret + np.where(n < max_exact, n, large)


def _bias_tables(rel_bias, half, mirror=False):
    sgn = -1 if mirror else 1
    i = np.arange(128)[:, None]
    c = np.arange(128)[None, :]
    A = np.empty((128, 8, 3, 512), np.float32)
    for bi, d in enumerate(DIL):
        rel_prev = 64 + i - c
        rel_cur = i - 64 - c
        for h in range(8):
            def tab(rel):
                v = rel_bias[_t5_bucket(sgn * rel * d), h]
                return np.where(np.abs(rel) <= 64, v, NEGB).astype(np.float32)
            tp, tcur = tab(rel_prev), tab(rel_cur)
            A[:, h, bi, 0:128] = tp
            A[:, h, bi, 128:256] = tcur
            first = tcur.copy()
            last = tp.copy()
            if half == 0:
                first[0:64, :] = NEGB
            else:
                last[64:128, :] = NEGB
            A[:, h, bi, 256:384] = first
            A[:, h, bi, 384:512] = last
    B = np.empty((128, 8, 640), np.float32)
    for h in range(8):
        def tabb(rel):
            v = rel_bias[_t5_bucket(sgn * rel), 8 + h]
            return np.where(np.abs(rel) <= 128, v, NEGB).astype(np.float32)
        t0, t1, t2 = tabb(128 + i - c), tabb(i - c), tabb(-128 + i - c)
        B[:, h, 0:128] = t0
        B[:, h, 128:256] = t1
        B[:, h, 256:384] = t2
        first = t2.copy()
        last = t0.copy()
        if half == 0:
            first[:, :] = NEGB
        else:
            last[:, :] = NEGB
        B[:, h, 384:512] = first
        B[:, h, 512:640] = last
    return A.reshape(128, -1), B.reshape(128, -1)


def _vecs(inp):
    v = np.zeros((128, 2 * VL), np.float32)
    for l in range(2):
        o = l * VL
        for name, col in (("norm_ffn1", V_NF1), ("norm_mix", V_NMIX), ("norm_ffn2", V_NF2), ("norm_ple", V_NPLE)):
            v[:, o + col:o + col + 8] = inp[name][l].reshape(8, 128).T
        for name, col in (("q_norm_a", V_QNA), ("k_norm_a", V_KNA), ("q_norm_b", V_QNB), ("k_norm_b", V_KNB)):
            v[:, o + col] = np.tile(inp[name][l], 2)
        for hb in range(8):
            v[:, o + V_SINK + hb] = inp["sink_b"][l, hb]
    return v


def _wqkv_dev(w):
    qa, ka, va, qb, kb, vb = (w[..., 0:512], w[..., 512:1024], w[..., 1024:1536], w[..., 1536:2048],
                              w[..., 2048:2176], w[..., 2176:2304])
    kbsw = np.concatenate([kb[..., 64:128], kb[..., 0:64]], axis=-1)
    return np.ascontiguousarray(np.concatenate([qa, ka, qb, kb, kbsw, va, vb], axis=-1))


def build(stages, first, last, fused=False):
    nc = bass.Bass("TRN2", target_bir_lowering=False)
    P = Prog(nc)
    if fused:
        stages = [('A', 0), ('B', 0), ('A', 1), ('B', 1)]
    layers_A = [l for (s, l) in stages if s == 'A']
    layers_B = [l for (s, l) in stages if s in ('B', 'Batt')]
    need_w = sorted(set(layers_A + layers_B))

    def din(name, shape, dt=F32):
        return nc.dram_tensor(name, shape, dt, kind="ExternalInput").ap()

    def dout(name, shape, dt=F32):
        return nc.dram_tensor(name, shape, dt, kind="ExternalOutput").ap()

    xT_in = din("xT_in", [D, T])
    if fused:
        xT_in1 = din("xT_in1", [D, T])
    vecs_d = din("vecs", [128, 2 * VL])
    ident_d = din("ident", [128, 128])
    W = {}
    for l in need_w:
        if l in layers_A:
            W[("w_in1", l)] = din(f"w_in1_{l}", [D, 2 * DFF])
            W[("w_out1", l)] = din(f"w_out1_{l}", [DFF, D])
            W[("w_qkv", l)] = din(f"w_qkv_{l}", [D, QKVW])
        if l in layers_B:
            W[("w_o", l)] = din(f"w_o_{l}", [D, D])
            W[("w_in2", l)] = din(f"w_in2_{l}", [D, 2 * DFF])
            W[("w_out2", l)] = din(f"w_out2_{l}", [DFF, D])
            W[("w_g", l)] = din(f"w_g_{l}", [D, D])
            W[("w_p", l)] = din(f"w_p_{l}", [256, D])
            W[("pT", l)] = din(f"pT_{l}", [256, T])
            if fused and l == 0:
                W[("pT1", l)] = din(f"pT1_{l}", [256, T])
    if layers_B:
        biasA_d = din("biasA", [128, 8 * 3 * 512])
        biasB_d = din("biasB", [128, 8 * 640])
        if fused:
            biasA1_d = din("biasA1", [128, 8 * 3 * 512])
            biasB1_d = din("biasB1", [128, 8 * 640])
    startB = stages[0][0] in ('B', 'Batt') and not fused
    dbg_att = stages[-1][0] == 'Batt'
    if startB:
        q_in = din("q_in", [D, T], BF16)
        kf_in = din("kf_in", [768, FR], BF16)
        vf_in = din("vf_in", [FR, 640], BF16)
    endA = stages[-1][0] == 'A'
    if fused:
        def dscr(name, shape, dt):
            return nc.dram_tensor(name, shape, dt).ap()
        xs0 = dscr("xs0", [D, T], F32)
        qs0 = dscr("qs0", [D, T], BF16)
        ks = {(l, t): dscr(f"ks_{l}_{t}", [768, FR], BF16) for l in range(2) for t in range(2)}
        vs = {(l, t): dscr(f"vs_{l}_{t}", [FR, 640], BF16) for l in range(2) for t in range(2)}
        kmid = {k_: v_[:, HALO:HALO + T] for k_, v_ in ks.items()}
        vmid = {k_: v_[HALO:HALO + T, :] for k_, v_ in vs.items()}
        kfr, vfr = ks, vs
    if endA or dbg_att:
        q_out = dout("q_out", [D, T], BF16)
        k_out = dout("k_out", [768, T], BF16)
        v_out = dout("v_out", [T, 640], BF16)
    xT_out = dout("xT_out", [D, T])

    with contextlib.ExitStack() as es:
        def sb(name, shape, dt):
            return es.enter_context(nc.sbuf_tensor(name, shape, dt))

        xT = sb("xT", [128, 8, T], F32)
        qS = sb("qS", [128, 8, T], BF16)
        vecs = sb("vecs_s", [128, 2 * VL], F32)
        vecs8 = sb("vecs8", [128, 2 * VL], F32)
        ident = sb("ident_s", [128, 128], BF16)
        ones = sb("ones", [128, 128], BF16)
        bones = sb("bones", [128, 128], BF16)
        epsb = sb("epsb", [128, 1], F32)
        ARENA = 108000
        arena = sb("arena", [128, ARENA // 2], BF16)
        ps = [es.enter_context(nc.psum_tensor(f"ps{k}", [128, 512], F32)) for k in range(8)]

        class Carver:
            def __init__(self):
                self.off = 0

            def take(self, shape, dt):
                n = int(np.prod(shape[1:]))
                nb = n * (4 if dt == F32 else 2)
                nb = (nb + 63) // 64 * 64
                assert self.off + nb <= ARENA, (self.off, nb)
                v = arena[:, self.off // 2:(self.off + nb) // 2]
                self.off += nb
                if dt == F32:
                    v = v.bitcast(F32)
                v = v[:, 0:n]
                if len(shape) == 3:
                    v = v.rearrange("p (a b) -> p a b", a=shape[1])
                elif len(shape) == 4:
                    v = v.rearrange("p (a b c) -> p a b c", a=shape[1], b=shape[2])
                return v

        P.op('dve', lambda e: e.memset(ones[:], 1.0), writes=['ones'])
        P.op('dve', lambda e: e.memset(epsb[:], EPS), writes=['epsb'])
        P.op('dve', lambda e: e.memset(bones[:], 0.0), writes=['bones'])
        P.op('dve', lambda e: e.memset(bones[0:64, 0:64], 1.0), reads=['bones'], writes=['bones'])
        P.op('dve', lambda e: e.memset(bones[64:128, 64:128], 1.0), reads=['bones'], writes=['bones'])
        P.op('pool', lambda e: e.dma_start(out=ident[:], in_=ident_d), writes=['ident'], dma='g_ident')
        P.op('sp', lambda e: e.dma_start(out=vecs[:], in_=vecs_d), writes=['vecs'], dma='g_vecs')
        P.op('dve', lambda e: e.tensor_scalar_mul(out=vecs8[:], in0=vecs[:], scalar1=0.125),
             reads=['vecs'], writes=['vecs8'])
        for l in range(2):
            c0 = l * VL + V_SINK
            P.op('act', lambda e, c0=c0: e.activation(out=vecs8[:, c0:c0 + 8], in_=vecs[:, c0:c0 + 8], func=AF.Exp),
                 reads=['vecs', 'vecs8'], writes=['vecs8'])
        def load_x(src):
            P.op('sp', lambda e: e.dma_start(out=xT[:, :, :], in_=src.rearrange("(dc p) t -> p dc t", p=128)),
                 writes=[('xT', dc, tg) for dc in range(8) for tg in range(NTG)], dma='g_xin')

        def load_q(src):
            P.op('sp', lambda e: e.dma_start(out=qS[:, :, :], in_=src.rearrange("(c p) t -> p c t", p=128)),
                 writes=[('q', c, h) for c in range(8) for h in range(2)], dma='g_qin')

        def save_x(dst, final=False):
            P.op('sp', lambda e: e.dma_start(out=dst.rearrange("(dc p) t -> p dc t", p=128), in_=xT[:, :, :]),
                 reads=[('xT', dc, tg) for dc in range(8) for tg in range(NTG)], dma='g_xout', final=final)

        def save_q(dst, final=False):
            P.op('sp', lambda e: e.dma_start(out=dst.rearrange("(c p) t -> p c t", p=128), in_=qS[:, :, :]),
                 reads=[('q', c, h) for c in range(8) for h in range(2)], dma='g_qout', final=final)

        def make_frames(l, t):
            kd, vd = ks[(l, t)], vs[(l, t)]
            if t == 0:
                kl, vl = kmid[(l, 0)][:, 0:HALO], vmid[(l, 0)][0:HALO, :]
            else:
                kl, vl = kmid[(l, 0)][:, T - HALO:T], vmid[(l, 0)][T - HALO:T, :]
            kr, vr = kmid[(l, 1)][:, 0:HALO], vmid[(l, 1)][0:HALO, :]
            for (dst, src) in ((kd[:, 0:HALO], kl), (kd[:, HALO + T:FR], kr), (vd[0:HALO, :], vl), (vd[HALO + T:FR, :], vr)):
                P.op('sp', lambda e, dst=dst, src=src: e.dma_start(out=dst, in_=src), dma='g_frames')

        if not fused:
            load_x(xT_in)
            if startB:
                load_q(q_in)

        def norm_phase(cv, hn, gcol, tgs=tuple(range(NTG))):
            sq = [cv.take([128, 512], BF16) for _ in range(3)]
            std = [cv.take([128, 512], F32) for _ in range(2)]
            k = 0
            for tg in tgs:
                ts = slice(tg * 512, (tg + 1) * 512)
                for dc in range(8):
                    s = k % 3
                    k += 1
                    P.op('act', lambda e, s=s, dc=dc, ts=ts: e.activation(out=sq[s], in_=xT[:, dc, ts], func=AF.Square,
                                                                          scale=1.0 / 32.0),
                         reads=[('xT', dc, tg)], writes=[('nsq', s)])
                    P.op('pe', lambda e, s=s, dc=dc: e.matmul(ps[7][:], ones[:], sq[s], start=(dc == 0), stop=(dc == 7)),
                         reads=[('nsq', s), 'ones'], writes=[('ps', 7)])
                b = tg % 2
                P.op('act', lambda e, b=b: e.activation(out=std[b], in_=ps[7][:], func=AF.Ln, bias=epsb[:, 0:1], scale=1.0),
                     reads=[('ps', 7), 'epsb'], writes=[('nstd', b)])
                P.op('act', lambda e, b=b: e.activation(out=std[b], in_=std[b], func=AF.Exp, scale=-0.5),
                     reads=[('nstd', b)], writes=[('nstd', b)])
                for dc in range(8):
                    P.op('dve', lambda e, b=b, dc=dc, ts=ts: e.scalar_tensor_tensor(
                        out=hn[:, dc, ts], in0=xT[:, dc, ts], scalar=vecs[:, gcol + dc:gcol + dc + 1], in1=std[b],
                        op0=ALU.mult, op1=ALU.mult),
                        reads=[('xT', dc, tg), ('nstd', b), 'vecs'], writes=[('hn', dc, tg)])

        def ffn_phase(l, w_in, w_out, gcol, tgs=tuple(range(NTG))):
            P.barrier()
            cv = Carver()
            hn = cv.take([128, 8, T], BF16)
            win = [cv.take([128, 8, 1024], BF16) for _ in range(2)]
            wout = [cv.take([128, 4, 1024], BF16) for _ in range(2)]
            act = [cv.take([128, 4, 512], BF16) for _ in range(2)]
            sg = [cv.take([128, 512], F32) for _ in range(2)]
            norm_phase(cv, hn, gcol, tgs)
            parts = [(f0, min(4, NFC - f0)) for f0 in range(0, NFC, 4)]
            w_in_v = w_in.rearrange("(dc p) f -> p dc f", p=128)
            w_out_v = w_out.rearrange("(fc p) d -> p fc d", p=128)

            def load_part(pi):
                f0, nf = parts[pi]
                s = pi % 2
                P.op('pool', lambda e: e.dma_start(out=win[s][:, :, 0:nf * 128], in_=w_in_v[:, :, f0 * 128:(f0 + nf) * 128]),
                     writes=[('wing', s)], dma=f'g_wing{s}')
                P.op('pool', lambda e: e.dma_start(out=win[s][:, :, 512:512 + nf * 128],
                                                   in_=w_in_v[:, :, DFF + f0 * 128:DFF + (f0 + nf) * 128]),
                     writes=[('winu', s)], dma=f'g_winu{s}')
                P.op('pool', lambda e: e.dma_start(out=wout[s][:, 0:nf, :], in_=w_out_v[:, f0:f0 + nf, :]),
                     writes=[('wout', s)], dma=f'g_wout{s}')

            jobs = [(pi, tg) for pi in range(len(parts)) for tg in tgs]
            gk = [0]
            yk = [0]

            def GU(ji):
                pi, tg = jobs[ji]
                f0, nf = parts[pi]
                s = pi % 2
                a = ji % 2
                ts = slice(tg * 512, (tg + 1) * 512)
                for fc in range(nf):
                    b = gk[0] % 2
                    gk[0] += 1
                    for dc in range(8):
                        P.op('pe', lambda e, dc=dc, fc=fc, b=b: e.matmul(ps[b][:], win[s][:, dc, fc * 128:(fc + 1) * 128],
                                                                         hn[:, dc, ts], start=(dc == 0), stop=(dc == 7)),
                             reads=[('wing', s), ('hn', dc, tg)], writes=[('ps', b)])
                    for dc in range(8):
                        P.op('pe', lambda e, dc=dc, fc=fc, b=b: e.matmul(ps[2 + b][:], win[s][:, dc, 512 + fc * 128:512 + (fc + 1) * 128],
                                                                         hn[:, dc, ts], start=(dc == 0), stop=(dc == 7)),
                             reads=[('winu', s), ('hn', dc, tg)], writes=[('ps', 2 + b)])
                    P.op('act', lambda e, b=b: e.activation(out=sg[b], in_=ps[b][:], func=AF.Silu),
                         reads=[('ps', b)], writes=[('sg', b)])
                    P.op('dve', lambda e, b=b, fc=fc: e.tensor_tensor(out=act[a][:, fc, :], in0=ps[2 + b][:], in1=sg[b], op=ALU.mult),
                         reads=[('ps', 2 + b), ('sg', b)], writes=[('act', a, fc)])

            def Y(ji):
                pi, tg = jobs[ji]
                f0, nf = parts[pi]
                s = pi % 2
                a = ji % 2
                ts = slice(tg * 512, (tg + 1) * 512)
                for dc in range(8):
                    b = 4 + yk[0] % 3
                    yk[0] += 1
                    for fc in range(nf):
                        P.op('pe', lambda e, dc=dc, fc=fc, b=b: e.matmul(ps[b][:], wout[s][:, fc, dc * 128:(dc + 1) * 128],
                                                                         act[a][:, fc, :], start=(fc == 0), stop=(fc == nf - 1)),
                             reads=[('wout', s), ('act', a, fc)], writes=[('ps', b)])
                    P.op('dve', lambda e, dc=dc, b=b: e.scalar_tensor_tensor(out=xT[:, dc, ts], in0=ps[b][:], scalar=0.5,
                                                                            in1=xT[:, dc, ts], op0=ALU.mult, op1=ALU.add),
                         reads=[('ps', b), ('xT', dc, tg)], writes=[('xT', dc, tg)])

            load_part(0)
            load_part(1)
            for ji in range(len(jobs) + 1):
                if ji < len(jobs):
                    pi, tg = jobs[ji]
                    GU(ji)
                if ji >= 1:
                    Y(ji - 1)
                    ppi, ptg = jobs[ji - 1]
                    if ptg == tgs[-1] and ppi + 2 < len(parts):
                        load_part(ppi + 2)

        def qkv_phase(l, k_out, v_out, final=True, tgs=tuple(range(NTG))):
            P.barrier()
            cv = Carver()
            hn = cv.take([128, 8, T], BF16)
            wq = cv.take([128, 8, QKVW], BF16)
            kst = [cv.take([128, T], BF16) for _ in range(2)]
            vst = [cv.take([128, 640], BF16) for _ in range(2)]
            sq = [cv.take([128, 512], BF16) for _ in range(4)]
            std = [cv.take([128, 512], F32) for _ in range(4)]
            wv = W[("w_qkv", l)].rearrange("(dc p) f -> p dc f", p=128)
            for (c0, c1) in ((0, 1024), (1024, 1792), (1792, QKVW)):
                P.op('pool', lambda e, c0=c0, c1=c1: e.dma_start(out=wq[:, :, c0:c1], in_=wv[:, :, c0:c1]),
                     writes=[('wq', c0)], dma=f'g_wq{c0}')
            norm_phase(cv, hn, l * VL + V_NMIX, tgs)
            vo = l * VL
            units = [(ch, tg) for ch in range(14) for tg in tgs]
            ncol = 512 * len(tgs)

            def unit_info(ch):
                if ch < 4:
                    return 'q', vecs8[:, vo + V_QNA:vo + V_QNA + 1], ch
                elif ch < 8:
                    return 'k', vecs[:, vo + V_KNA:vo + V_KNA + 1], ch - 4
                elif ch < 12:
                    return 'q', vecs8[:, vo + V_QNB:vo + V_QNB + 1], ch - 4
                return 'k', vecs[:, vo + V_KNB:vo + V_KNB + 1], ch - 8

            def proj(ui):
                ch, tg = units[ui]
                b = ui % 4
                ts = slice(tg * 512, (tg + 1) * 512)
                wres = ('wq', 0) if ch < 8 else ('wq', 1024)
                for dc in range(8):
                    P.op('pe', lambda e, dc=dc: e.matmul(ps[b][:], wq[:, dc, ch * 128:(ch + 1) * 128], hn[:, dc, ts],
                                                         start=(dc == 0), stop=(dc == 7)),
                         reads=[wres, ('hn', dc, tg)], writes=[('ps', b)])
                P.op('act', lambda e: e.activation(out=sq[b], in_=ps[b][:], func=AF.Square, scale=0.125),
                     reads=[('ps', b)], writes=[('qsq', b)])

            def stats(ui):
                ch, tg = units[ui]
                b = ui % 4
                sb_ = 4 + ui % 2
                ts = slice(tg * 512, (tg + 1) * 512)
                kind, gv, dst = unit_info(ch)
                P.op('pe', lambda e: e.matmul(ps[sb_][:], bones[:], sq[b], start=True, stop=True),
                     reads=[('qsq', b), 'bones'], writes=[('ps', sb_)])
                P.op('act', lambda e: e.activation(out=std[b], in_=ps[sb_][:], func=AF.Ln, bias=epsb[:, 0:1], scale=1.0),
                     reads=[('ps', sb_), 'epsb'], writes=[('qstd', b)])
                P.op('act', lambda e: e.activation(out=std[b], in_=std[b], func=AF.Exp, scale=-0.5),
                     reads=[('qstd', b)], writes=[('qstd', b)])
                if kind == 'q':
                    P.op('dve', lambda e: e.scalar_tensor_tensor(
                        out=qS[:, dst, ts], in0=ps[b][:], scalar=gv, in1=std[b], op0=ALU.mult, op1=ALU.mult),
                        reads=[('ps', b), ('qstd', b), 'vecs', 'vecs8'], writes=[('q', dst, 0), ('q', dst, 1)])
                else:
                    ksl = dst % 2
                    P.op('dve', lambda e: e.scalar_tensor_tensor(
                        out=kst[ksl][:, ts], in0=ps[b][:], scalar=gv, in1=std[b], op0=ALU.mult, op1=ALU.mult),
                        reads=[('ps', b), ('qstd', b), 'vecs'], writes=[('kst', ksl)])
                    if tg == tgs[-1]:
                        P.op('sp', lambda e: e.dma_start(out=k_out[dst * 128:(dst + 1) * 128, 0:ncol], in_=kst[ksl][:, 0:ncol]),
                             reads=[('kst', ksl)], dma=f'g_kout{ksl}', final=final)

            LAQ = 2
            for ui in range(len(units) + LAQ):
                if ui < len(units):
                    proj(ui)
                if ui >= LAQ:
                    stats(ui - LAQ)
            for tt in range(4 * len(tgs)):
                tsl = slice(tt * 128, (tt + 1) * 128)
                b = tt % 2
                for dc in range(8):
                    P.op('pe', lambda e, dc=dc, b=b, tsl=tsl: e.matmul(ps[4 + b][:], hn[:, dc, tsl], wq[:, dc, 1792:2304],
                                                                       start=(dc == 0), stop=(dc == 7)),
                         reads=[('wq', 1792), ('hn', dc, tt // 4)], writes=[('ps', 4 + b)])
                for dc in range(8):
                    P.op('pe', lambda e, dc=dc, b=b, tsl=tsl: e.matmul(ps[6 + b][:, 0:128], hn[:, dc, tsl], wq[:, dc, 2304:2432],
                                                                       start=(dc == 0), stop=(dc == 7)),
                         reads=[('wq', 1792), ('hn', dc, tt // 4)], writes=[('ps', 6 + b)])
                P.op('act', lambda e, b=b: e.copy(out=vst[b][:, 0:512], in_=ps[4 + b][:]),
                     reads=[('ps', 4 + b)], writes=[('vst', b)])
                P.op('dve', lambda e, b=b: e.tensor_copy(out=vst[b][:, 512:640], in_=ps[6 + b][:, 0:128]),
                     reads=[('ps', 6 + b)], writes=[('vstb', b)])
                P.op('sp', lambda e, b=b, tsl=tsl: e.dma_start(out=v_out[tsl, :], in_=vst[b]),
                     reads=[('vst', b), ('vstb', b)], dma=f'g_vout{b}', final=final)

        def att_phase(l, kf_d, vf_d, biasA_d, biasB_d, qhalf=False):
            P.barrier()
            cv = Carver()
            kfb = [cv.take([128, FR], BF16) for _ in range(2)]
            NTA = 17 + 20 + 32
            VtA = cv.take([128, NTA, 2, 128], BF16)
            VtB = cv.take([128, 18, 2, 128], BF16)
            biasb = [cv.take([128, 3072], BF16) for _ in range(2)]
            NPT = 8
            PT = [cv.take([128, 512], BF16) for _ in range(NPT)]
            SBANKS = (0, 1, 2, 5, 6, 7)
            Uacc = [cv.take([128, T], F32) for _ in range(2)]
            rz = cv.take([128, T], F32)
            vo = l * VL
            P.op('dve', lambda e: e.memset(VtA[:, :, :, 64:128], 1.0), writes=['Vones'])
            P.op('dve', lambda e: e.memset(VtB[:, :, :, 64:128], 1.0), reads=['Vones'], writes=['Vones'])
            tbase = {}
            n = 0
            for bi, d in enumerate(DIL):
                nqb = 16 // d
                for r in range(d):
                    tbase[(bi, r)] = n
                    n += nqb + 1
            assert n == NTA
            sk = [0]
            ok = [0]

            stream = []

            def run_head(ch, hp, h, jobs, Ua, vt, ku, bs, pre=(), post=()):
                for job in jobs:
                    job.update(ch=ch, hp=hp, Ua=Ua, vtens=vt, ku=ku, bs=bs, pre=[], post=[])
                jobs[0]['pre'] = list(pre)
                jobs[-1]['post'] = list(post)
                stream.extend(jobs)

            def S(job):
                b = SBANKS[sk[0] % 6]
                pt = sk[0] % NPT
                job['pt'] = pt
                sk[0] += 1
                job['sb'] = b
                N = job['N']
                ch, hp, ku, bs = job['ch'], job['hp'], job['ku'], job['bs']
                bias = biasb[bs]
                P.op('pe', lambda e: e.matmul(ps[b][:, 0:N], job['k'], job['q'], start=True, stop=True),
                     reads=[('kf', ku), ('q', ch, hp)], writes=[('ps', b)])
                P.op('act', lambda e: e.activation(out=PT[pt][:, 0:N], in_=ps[b][:, 0:N], func=AF.Exp),
                     reads=[('ps', b)], writes=[('PT', pt)])
                P.op('dve', lambda e: e.tensor_tensor(out=PT[pt][:, 0:N], in0=PT[pt][:, 0:N], in1=bias[:, job['bc']:job['bc'] + N], op=ALU.mult),
                     reads=[('PT', pt), ('bias', bs)], writes=[('PT', pt)])

            def PV(job):
                b = job['sb']
                hp, vt = job['hp'], job['vtens']
                blks = job['blocks']
                (col0, grp, slot0, st, _sp, _f) = blks[0]
                nb = len(blks)
                for k_, bl in enumerate(blks):
                    assert bl[2] == slot0 + k_ and bl[0] == col0 + 128 * k_
                ob = 3 + grp % 2
                fin = blks[-1][5]
                P.op('pe', lambda e: e.matmul(
                    ps[ob][:, slot0 * 128:(slot0 + nb) * 128], vt[:, job['vt'], job['vh'], :],
                    PT[job['pt']][:, col0:col0 + 128 * nb], start=st, stop=True, skip_group_check=True),
                    reads=[('PT', job['pt']), job['vres'], 'Vones'], writes=[('ps', ob)])
                if fin is not None:
                    dst, src_ap, first = fin
                    if first:
                        P.op('dve', lambda e: e.tensor_copy(out=dst, in_=src_ap(ps[ob])),
                             reads=[('ps', ob)], writes=[('Ua', hp)])
                    else:
                        P.op('dve', lambda e: e.tensor_tensor(out=dst, in0=src_ap(ps[ob]), in1=dst, op=ALU.add),
                             reads=[('ps', ob), ('Ua', hp)], writes=[('Ua', hp)])

            def flush_stream():
                LA = 7
                DEFER = 6
                pending = []
                for i in range(len(stream) + LA):
                    if i < len(stream):
                        for f in stream[i]['pre']:
                            f()
                        S(stream[i])
                    if i >= LA:
                        job = stream[i - LA]
                        PV(job)
                        for f in job['post']:
                            f()
                        for k_, f in enumerate(job.get('fin', ())):
                            pending.append((i + DEFER + k_, f))
                    if pending and pending[0][0] <= i:
                        pending.pop(0)[1]()
                for (_r, f) in pending:
                    f()

            gcount = [0]

            def make_jobs_A(ch, hp, Ua, kf):
                jobs = []
                qv = qS[hp * 64:(hp + 1) * 64, ch, :]
                kv = kf[hp * 64:(hp + 1) * 64, :]
                bh = hp * 1536
                for bi, d in enumerate(DIL):
                    nqb = 16 // d
                    if d == 16:
                        groups = [[(r, 0) for r in range(g * 4, g * 4 + 4)] for g in range(4)]
                    elif d == 4:
                        groups = [[(r, b) for b in range(2 if qhalf else 4)] for r in range(4)]
                    else:
                        groups = [[(0, b) for b in range(g * 4, g * 4 + 4)] for g in range(2 if qhalf else 4)]
                    for grp_blocks in groups:
                        grp = gcount[0]
                        gcount[0] += 1
                        slot_of = {rb: s for s, rb in enumerate(grp_blocks)}
                        classes = sorted(set(r for r, _ in grp_blocks))
                        if d == 1:
                            b0 = grp_blocks[0][1]
                            dst = Ua[:, b0 * 128:b0 * 128 + 512]
                            src_ap = (lambda p: p[:, 0:512])
                        elif d == 4:
                            r = grp_blocks[0][0]
                            nm = 128 * len(grp_blocks)
                            dst = Ua[:, :].rearrange("p (m r) -> p r m", r=4)[:, r, 0:nm]
                            src_ap = (lambda p, nm=nm: p[:, 0:nm])
                        else:
                            r0 = grp_blocks[0][0]
                            dst = Ua[:, :].rearrange("p (m r) -> p r m", r=16)[:, r0:r0 + 4, :]
                            src_ap = (lambda p: p[:, 0:512].rearrange("p (r m) -> p r m", r=4))
                        first = (bi == 0)
                        pending = []
                        for r in classes:
                            blks = sorted(b for (rr, b) in grp_blocks if rr == r)
                            tiles = sorted(set([b for b in blks] + [b + 1 for b in blks]))
                            for j in tiles:
                                served = [b for b in (j - 1, j) if b in blks]
                                bq0 = served[0]
                                N = 128 * len(served)
                                kstart = HALO + d * (128 * j - 64) + r
                                kap = kv.rearrange("p (m r) -> p r m", r=d)[:, kstart % d, kstart // d:kstart // d + 128]
                                qstart = d * 128 * bq0 + r
                                qap = qv.rearrange("p (m r) -> p r m", r=d)[:, qstart % d, qstart // d:qstart // d + N]
                                if j == 0:
                                    bc = bh + bi * 512 + 256
                                elif j == nqb:
                                    bc = bh + bi * 512 + 384
                                else:
                                    bc = bh + bi * 512 + (0 if served[0] == j - 1 else 128)
                                blocks = []
                                for si, b in enumerate(served):
                                    st = (len(pending) == 0)
                                    pending.append(1)
                                    sp_ = (j == b + 1)
                                    blocks.append([si * 128, grp, slot_of[(r, b)], st, sp_, None])
                                jobs.append({'k': kap, 'q': qap, 'N': N, 'bc': bc, 'vt': tbase[(bi, r)] + j, 'vh': hp,
                                             'blocks': blocks, 'vres': ('Vt', bi, hp)})
                        jobs[-1]['blocks'][-1][5] = (dst, src_ap, first)
                return jobs

            def make_jobs_B(ch, hp, Ua, kvh, kf):
                jobs = []
                qv = qS[hp * 64:(hp + 1) * 64, ch, :]
                kv = kf[hp * 64:(hp + 1) * 64, :]
                bh = hp * 640
                for g in range(2 if qhalf else 4):
                    grp = gcount[0]
                    gcount[0] += 1
                    blks = list(range(g * 4, g * 4 + 4))
                    tiles = list(range(blks[0], blks[-1] + 3))
                    for j in tiles:
                        served = [b for b in (j - 2, j - 1, j) if b in blks]
                        N = 128 * len(served)
                        kstart = HALO + 128 * (j - 1)
                        kap = kv[:, kstart:kstart + 128]
                        qap = qv[:, served[0] * 128:served[0] * 128 + N]
                        if j == 0:
                            bc = bh + 384
                        elif j == 17:
                            bc = bh + 512
                        else:
                            bc = bh + 128 * (served[0] - (j - 2))
                        blocks = []
                        for si, b in enumerate(served):
                            blocks.append([si * 128, grp, b - blks[0], (j == tiles[0] and si == 0), (j == b + 2), None])
                        jobs.append({'k': kap, 'q': qap, 'N': N, 'bc': bc, 'vt': j, 'vh': kvh, 'blocks': blocks, 'vres': ('VtB', kvh)})
                    dst = Ua[:, blks[0] * 128:blks[0] * 128 + 512]
                    jobs[-1]['blocks'][-1][5] = (dst, (lambda p: p[:, 0:512]), True)
                return jobs

            def finish_head(ch, hp, Ua, sinkcol):
                hs = slice(hp * 64, (hp + 1) * 64)
                nq = T // 2 if qhalf else T
                pieces = []
                if sinkcol is not None:
                    pieces.append(lambda: P.op('act', lambda e: e.activation(out=rz[64:128, 0:nq], in_=Ua[64:128, 0:nq], func=AF.Ln,
                                                                             bias=vecs8[64:128, sinkcol:sinkcol + 1], scale=1.0),
                                               reads=[('Ua', hp), 'vecs8'], writes=[('rz',)]))
                else:
                    pieces.append(lambda: P.op('act', lambda e: e.activation(out=rz[64:128, 0:nq], in_=Ua[64:128, 0:nq], func=AF.Ln),
                                               reads=[('Ua', hp)], writes=[('rz',)]))
                pieces.append(lambda: P.op('act', lambda e: e.activation(out=rz[64:128, 0:nq], in_=rz[64:128, 0:nq], func=AF.Exp, scale=-1.0),
                                           reads=[('rz',)], writes=[('rz',)]))
                nchunk = 4
                w = nq // nchunk
                for k_ in range(nchunk):
                    cs = slice(k_ * w, (k_ + 1) * w)
                    pieces.append(lambda cs=cs, k_=k_: P.op('dve', lambda e: e.tensor_copy(out=rz[0:64, cs], in_=rz[64:128, cs]),
                                                            reads=[('rz',)], writes=[('rzs', k_)]))
                    pieces.append(lambda cs=cs, k_=k_: P.op('dve', lambda e: e.tensor_tensor(out=qS[hs, ch, cs], in0=Ua[0:64, cs], in1=rz[0:64, cs], op=ALU.mult),
                                                            reads=[('Ua', hp), ('rzs', k_)], writes=[('q', ch, hp)]))
                return pieces

            units = [('A', ch, None) for ch in range(4)] + [('B', ch, hp) for ch in range(4, 8) for hp in range(2)]

            def kf_bias_dma(u):
                kind, ch, hp = units[u]
                ku = u % 2
                if kind == 'A':
                    P.op('sp', lambda e: e.dma_start(out=kfb[ku][:, :], in_=kf_d[ch * 128:(ch + 1) * 128, :]),
                         writes=[('kf', ku)], dma=f'g_kf{ku}')
                    bs = ch % 2
                    P.op('pool', lambda e: e.dma_start(out=biasb[bs][:, 0:3072], in_=biasA_d[:, ch * 3072:(ch + 1) * 3072]),
                         writes=[('biasraw', bs), ('bias', bs)], dma=f'g_bias{bs}')
                else:
                    hb = (ch - 4) * 2 + hp
                    kvh = hb // 4
                    krow = 512 if kvh == hp else 640
                    P.op('sp', lambda e: e.dma_start(out=kfb[ku][:, :], in_=kf_d[krow:krow + 128, :]),
                         writes=[('kf', ku)], dma=f'g_kf{ku}')
                    if hp == 0:
                        bs = ch % 2
                        P.op('pool', lambda e: e.dma_start(out=biasb[bs][:, 0:1280], in_=biasB_d[:, (ch - 4) * 1280:(ch - 4 + 1) * 1280]),
                             writes=[('biasraw', bs), ('bias', bs)], dma=f'g_bias{bs}')

            def bias_exp(u):
                kind, ch, hp = units[u]
                if kind == 'B' and hp == 1:
                    return
                bs = ch % 2
                n = 3072 if kind == 'A' else 1280
                P.op('act', lambda e: e.activation(out=biasb[bs][:, 0:n], in_=biasb[bs][:, 0:n], func=AF.Exp),
                     reads=[('biasraw', bs)], writes=[('bias', bs)])

            def v_loads(ch):
                for hh in range(2):
                    for bi, d in enumerate(DIL):
                        nqb = 16 // d
                        for r in range(d):
                            for tlo, thi in ([(0, 9), (9, 17)] if d == 1 else [(0, nqb + 1)]):
                                nt = thi - tlo
                                row0 = HALO + d * (128 * tlo - 64) + r
                                t0 = tbase[(bi, r)] + tlo
                                src = bass.AP(vf_d.tensor, row0 * 640 + ch * 128 + hh * 64,
                                              [[d * 640, 128], [d * 128 * 640, nt], [1, 64]])
                                P.op('sp', lambda e, src=src, t0=t0, nt=nt, hh=hh: e.dma_start(out=VtA[:, t0:t0 + nt, hh, 0:64], in_=src),
                                     also_writes=[('Vt', bi, hh)], dma=f'g_vt{bi}{hh}')

            for hh in range(2):
                src = bass.AP(vf_d.tensor, (HALO - 128) * 640 + 512 + hh * 64, [[640, 128], [128 * 640, 18], [1, 64]])
                P.op('sp', lambda e, src=src, hh=hh: e.dma_start(out=VtB[:, :, hh, 0:64], in_=src), writes=[('VtB', hh)], dma=f'g_vtb{hh}')
            kf_bias_dma(0)
            v_loads(0)
            for u, (kind, ch, hp_) in enumerate(units):
                pre = []
                if u + 1 < len(units):
                    pre.append(lambda u=u: kf_bias_dma(u + 1))
                pre.append(lambda u=u: bias_exp(u))
                ku = u % 2
                bs = ch % 2
                if kind == 'A':
                    for hp in range(2):
                        Ua = Uacc[hp]
                        jobs = make_jobs_A(ch, hp, Ua, kfb[ku])
                        post = []
                        if hp == 1 and ch + 1 < 4:
                            post.append(lambda ch=ch: v_loads(ch + 1))
                        run_head(ch, hp, None, jobs, Ua, VtA, ku, bs, pre=(pre if hp == 0 else ()), post=post)
                        jobs[-1]['fin'] = finish_head(ch, hp, Ua, None)
                else:
                    hp = hp_
                    Ua = Uacc[hp]
                    hb = (ch - 4) * 2 + hp
                    kvh = hb // 4
                    jobs = make_jobs_B(ch, hp, Ua, kvh, kfb[ku])
                    run_head(ch, hp, None, jobs, Ua, VtB, ku, bs, pre=pre, post=[])
                    jobs[-1]['fin'] = finish_head(ch, hp, Ua, vo + V_SINK + hb)
            flush_stream()

        def wo_phase(l, tgs=tuple(range(NTG))):
            P.barrier()
            cv = Carver()
            wo = cv.take([128, 8, D], BF16)
            wv = W[("w_o", l)].rearrange("(c p) d -> p c d", p=128)
            for h in range(2):
                P.op('pool', lambda e, h=h: e.dma_start(out=wo[:, 4 * h:4 * h + 4, :], in_=wv[:, 4 * h:4 * h + 4, :]),
                     writes=[('wo', h)], dma=f'g_wo{h}')
            k = 0
            for tg in tgs:
                ts = slice(tg * 512, (tg + 1) * 512)
                for dc in range(8):
                    b = k % 4
                    k += 1
                    for c in range(8):
                        P.op('pe', lambda e, c=c, dc=dc, b=b, ts=ts: e.matmul(ps[b][:], wo[:, c, dc * 128:(dc + 1) * 128], qS[:, c, ts],
                                                                              start=(c == 0), stop=(c == 7)),
                             reads=[('wo', c // 4), ('q', c, 0), ('q', c, 1)], writes=[('ps', b)])
                    P.op('dve', lambda e, dc=dc, b=b, ts=ts: e.tensor_tensor(out=xT[:, dc, ts], in0=ps[b][:], in1=xT[:, dc, ts], op=ALU.add),
                         reads=[('ps', b), ('xT', dc, tg)], writes=[('xT', dc, tg)])

        def ple_phase(l, pT_d, tgs=tuple(range(NTG))):
            P.barrier()
            cv = Carver()
            hn = cv.take([128, 8, T], BF16)
            wg = cv.take([128, 8, D], BF16)
            wp = cv.take([128, 2, D], BF16)
            pT = cv.take([128, 2, T], BF16)
            sgm = [cv.take([128, 512], F32) for _ in range(2)]
            wgv = W[("w_g", l)].rearrange("(c p) d -> p c d", p=128)
            wpv = W[("w_p", l)].rearrange("(c p) d -> p c d", p=128)
            ptv = pT_d.rearrange("(c p) t -> p c t", p=128)
            for h in range(2):
                P.op('pool', lambda e, h=h: e.dma_start(out=wg[:, 4 * h:4 * h + 4, :], in_=wgv[:, 4 * h:4 * h + 4, :]),
                     writes=[('wg', h)], dma=f'g_wg{h}')
            P.op('pool', lambda e: e.dma_start(out=wp[:, :, :], in_=wpv), writes=[('wp',)], dma='g_wp')
            P.op('pool', lambda e: e.dma_start(out=pT[:, :, :], in_=ptv), writes=[('pT',)], dma='g_pT')
            norm_phase(cv, hn, l * VL + V_NPLE, tgs)
            k = 0
            for tg in tgs:
                ts = slice(tg * 512, (tg + 1) * 512)
                for dc in range(8):
                    b = k % 2
                    k += 1
                    for c in range(8):
                        P.op('pe', lambda e, c=c, dc=dc, b=b, ts=ts: e.matmul(ps[b][:], wg[:, c, dc * 128:(dc + 1) * 128], hn[:, c, ts],
                                                                              start=(c == 0), stop=(c == 7)),
                             reads=[('wg', c // 4), ('hn', c, tg)], writes=[('ps', b)])
                    for c in range(2):
                        P.op('pe', lambda e, c=c, dc=dc, b=b, ts=ts: e.matmul(ps[2 + b][:], wp[:, c, dc * 128:(dc + 1) * 128], pT[:, c, ts],
                                                                              start=(c == 0), stop=(c == 1)),
                             reads=[('wp',), ('pT',)], writes=[('ps', 2 + b)])
                    P.op('act', lambda e, b=b: e.activation(out=sgm[b], in_=ps[b][:], func=AF.Sigmoid),
                         reads=[('ps', b)], writes=[('sgm', b)])
                    P.op('dve', lambda e, b=b: e.tensor_tensor(out=sgm[b], in0=ps[2 + b][:], in1=sgm[b], op=ALU.mult),
                         reads=[('ps', 2 + b), ('sgm', b)], writes=[('sgm', b)])
                    P.op('dve', lambda e, b=b, dc=dc, ts=ts: e.tensor_tensor(out=xT[:, dc, ts], in0=sgm[b], in1=xT[:, dc, ts], op=ALU.add),
                         reads=[('sgm', b), ('xT', dc, tg)], writes=[('xT', dc, tg)])

        ALLTG = tuple(range(NTG))

        def stage_A(l, k_dst, v_dst, final, tgs=ALLTG):
            ffn_phase(l, W[("w_in1", l)], W[("w_out1", l)], l * VL + V_NF1, tgs)
            qkv_phase(l, k_dst, v_dst, final, tgs)

        def stage_B(l, kf_d, vf_d, bA, bB, pT_d, tgs=ALLTG):
            att_phase(l, kf_d, vf_d, bA, bB, qhalf=(len(tgs) < NTG))
            wo_phase(l, tgs)
            ffn_phase(l, W[("w_in2", l)], W[("w_out2", l)], l * VL + V_NF2, tgs)
            ple_phase(l, pT_d, tgs)

        if not fused:
            for (s, l) in stages:
                if s == 'A':
                    stage_A(l, k_out, v_out, True)
                elif s == 'Batt':
                    att_phase(l, kf_in, vf_in, biasA_d, biasB_d)
                else:
                    stage_B(l, kf_in, vf_in, biasA_d, biasB_d, W[("pT", l)])
            P.barrier()
            save_x(xT_out, final=True)
            if endA or dbg_att:
                save_q(q_out, final=True)
        else:
            load_x(xT_in)
            stage_A(0, kmid[(0, 0)], vmid[(0, 0)], False)
            P.barrier()
            save_x(xs0)
            save_q(qs0)
            P.barrier()
            load_x(xT_in1)
            stage_A(0, kmid[(0, 1)], vmid[(0, 1)], False)
            P.barrier()
            make_frames(0, 1)
            make_frames(0, 0)
            stage_B(0, kfr[(0, 1)], vfr[(0, 1)], biasA1_d, biasB1_d, W[("pT1", 0)], tgs=(0, 1))
            stage_A(1, kmid[(1, 1)], vmid[(1, 1)], False, tgs=(0, 1))
            P.barrier()
            load_x(xs0)
            load_q(qs0)
            stage_B(0, kfr[(0, 0)], vfr[(0, 0)], biasA_d, biasB_d, W[("pT", 0)])
            stage_A(1, kmid[(1, 0)], vmid[(1, 0)], False)
            P.barrier()
            make_frames(1, 0)
            stage_B(1, kfr[(1, 0)], vfr[(1, 0)], biasA_d, biasB_d, W[("pT", 1)])
            P.barrier()
            save_x(xT_out, final=True)
        P.emit()
    return nc


_PROGS = {}


def _get_prog(key, stages, first, last, fused=False):
    if key not in _PROGS:
        _PROGS[key] = build(stages, first, last, fused=fused)
    return _PROGS[key]


def _frames(k_out, v_out):
    kfs, vfs = [], []
    for c in range(NCORES):
        half = c % 2
        kf = np.zeros((768, FR), k_out[c].dtype)
        vf = np.zeros((FR, 640), v_out[c].dtype)
        kf[:, HALO:HALO + T] = k_out[c]
        vf[HALO:HALO + T] = v_out[c]
        if half == 0:
            kf[:, HALO + T:] = k_out[c + 1][:, 0:HALO]
            vf[HALO + T:] = v_out[c + 1][0:HALO]
        else:
            kf[:, 0:HALO] = k_out[c - 1][:, T - HALO:T]
            vf[0:HALO] = v_out[c - 1][T - HALO:T]
        kfs.append(kf)
        vfs.append(vf)
    return kfs, vfs


def kernel(**inp):
    inp = {k: np.asarray(v) for k, v in inp.items()}
    x, p = inp["x"], inp["p"]
    vecs = _vecs(inp)
    wqkv = _wqkv_dev(inp["w_qkv"])
    tabs = [[_bias_tables(inp["rel_bias"], t, mirror=bool(m)) for t in range(2)] for m in range(2)]
    identh = np.eye(128, dtype=np.float32)
    cores = list(range(NCORES))
    nc = _get_prog("fused", None, True, True, fused=True)
    maps = []
    S = 2 * T
    for c in cores:
        b, half = c // 2, c % 2
        order = np.arange(S) if half == 0 else np.arange(S - 1, -1, -1)
        own, oth = order[0:T], order[T:S]
        tb = tabs[half]
        m = dict(xT_in=np.ascontiguousarray(x[b, own, :].T), xT_in1=np.ascontiguousarray(x[b, oth, :].T),
                 vecs=vecs, ident=identh,
                 biasA=tb[0][0], biasB=tb[0][1], biasA1=tb[1][0], biasB1=tb[1][1],
                 pT_0=np.ascontiguousarray(p[0, b, own, :].T), pT1_0=np.ascontiguousarray(p[0, b, oth, :].T),
                 pT_1=np.ascontiguousarray(p[1, b, own, :].T))
        for l in range(2):
            m.update({f"w_in1_{l}": inp["ffn1_w_in"][l], f"w_out1_{l}": inp["ffn1_w_out"][l], f"w_qkv_{l}": wqkv[l],
                      f"w_o_{l}": inp["w_o"][l], f"w_in2_{l}": inp["ffn2_w_in"][l], f"w_out2_{l}": inp["ffn2_w_out"][l],
                      f"w_g_{l}": inp["w_ple_gate"][l], f"w_p_{l}": inp["w_ple_proj"][l]})
        maps.append(m)
    res = run_bass_kernel_spmd(nc, maps, core_ids=cores).results
    out = np.empty((4, S, D), np.float32)
    for c in cores:
        b, half = c // 2, c % 2
        order = np.arange(S) if half == 0 else np.arange(S - 1, -1, -1)
        out[b, order[0:T], :] = res[c]["xT_out"].T
    return out
```

```python
import contextlib
import numpy as np
import ml_dtypes
import concourse.bass as bass
import concourse.mybir as mybir
from concourse.bass_utils import run_bass_kernel_spmd

F32 = mybir.dt.float32
BF16 = mybir.dt.bfloat16
ALU = mybir.AluOpType
AF = mybir.ActivationFunctionType

D = 1024
T = 2048
NTG = 4
DFF = 2816
NFC = 22
HALO = 1024
FR = T + 2 * HALO
QKVW = 2432
NEGB = -30000.0
EPS = 1e-6
DIL = (1, 4, 16)
NCORES = 8

V_NF1, V_NMIX, V_NF2, V_NPLE = 0, 8, 16, 24
V_QNA, V_KNA, V_QNB, V_KNB = 32, 33, 34, 35
V_SINK = 36
VL = 44


class Prog:
    ENG = ('pe', 'act', 'dve', 'pool', 'sp')

    def __init__(self, nc):
        self.nc = nc
        self.ops = []
        self.res = {}
        self.pending_barrier = None

    def op(self, eng, fn, reads=(), writes=(), dma=None, final=False, also_writes=()):
        idx = len(self.ops)
        deps = set()
        for w in also_writes:
            st = self.res.setdefault(w, [None, []])
            for r in st[1]:
                if r != idx:
                    deps.add((r, True))
            st[0] = idx
            st[1] = []
        for r in reads:
            st = self.res.setdefault(r, [None, []])
            if st[0] is not None:
                deps.add((st[0], False))
            st[1].append(idx)
        for w in writes:
            st = self.res.setdefault(w, [None, []])
            if st[0] is not None:
                deps.add((st[0], False))
            for r in st[1]:
                if r != idx:
                    deps.add((r, True))
            st[0] = idx
            st[1] = []
        self.ops.append({'eng': eng, 'fn': fn, 'dma': dma, 'sig': False, 'deps': deps, 'final': final})
        return idx

    def barrier(self):
        self.ops.append({'eng': None, 'barrier': True, 'dma': None, 'sig': False, 'deps': set(), 'final': False})
        self.res = {}

    def emit(self):
        nc = self.nc
        ops = self.ops
        last_eng = {}
        last_dma = {}
        pend = {}
        for i, op in enumerate(ops):
            if op.get('barrier'):
                snap = set(last_eng.values()) | set(last_dma.values())
                for e in self.ENG:
                    pend[e] = set(pend.get(e, set())) | snap
                continue
            e = op['eng']
            if e in pend and pend[e]:
                for d in pend[e]:
                    op['deps'].add((d, False))
                pend[e] = set()
            if op['dma'] is not None:
                last_dma[op['dma']] = i
            else:
                last_eng[e] = i
        for op in ops:
            if op.get('barrier'):
                continue
            waits = []
            for (d, war) in op['deps']:
                p = ops[d]
                if p['dma'] is None and op['dma'] is None and p['eng'] == op['eng']:
                    if p['eng'] == 'pe' or war:
                        continue
                if p['dma'] is None:
                    p['sig'] = True
                waits.append(d)
            op['waits'] = waits
        tick = {e: 0 for e in self.ENG}
        gcount = {}
        semnames = set()
        for op in ops:
            if op.get('barrier'):
                continue
            if op['dma'] is not None:
                g = op['dma']
                gcount[g] = gcount.get(g, 0) + 16
                op['sigval'] = (('dma', g), gcount[g])
                semnames.add(op['sigval'][0])
            elif op['sig']:
                tick[op['eng']] += 1
                op['sigval'] = (('eng', op['eng']), tick[op['eng']])
                semnames.add(op['sigval'][0])
        with contextlib.ExitStack() as es:
            sems = {}
            for sn in sorted(semnames, key=str):
                sems[sn] = es.enter_context(nc.semaphore("s_" + "_".join(map(str, sn))))
            block = es.enter_context(nc.Block())
            engobj = {'pe': 'tensor', 'act': 'scalar', 'dve': 'vector', 'pool': 'gpsimd', 'sp': 'sync'}
            finals = {}
            for op in ops:
                if op.get('final'):
                    sn, val = op['sigval']
                    finals[sn] = max(finals.get(sn, 0), val)

            def make(ename):
                def body(e):
                    waited = {}
                    for op in ops:
                        if op.get('barrier') or op['eng'] != ename:
                            continue
                        for d in op['waits']:
                            sn, val = ops[d]['sigval']
                            if waited.get(sn, 0) >= val:
                                continue
                            waited[sn] = val
                            e.wait_ge(sems[sn], val)
                        ins = op['fn'](e)
                        if op['dma'] is not None:
                            ins.then_inc(sems[op['sigval'][0]], 16)
                        elif op['sig']:
                            ins.then_inc(sems[op['sigval'][0]], 1)
                    if ename == 'sp':
                        for sn, val in finals.items():
                            e.wait_ge(sems[sn], val)
                return body
            for ename in self.ENG:
                getattr(block, engobj[ename])(make(ename))


def _t5_bucket(rel):
    half, max_exact = 16, 8
    ret = np.where(rel > 0, half, 0)
    n = np.abs(rel)
    nf = np.maximum(n, 1).astype(np.float32)
    large = max_exact + (np.log(nf / np.float32(max_exact)) / np.float32(np.log(1024 / max_exact))
                         * np.float32(half - max_exact)).astype(np.int32)
    large = np.minimum(large, half - 1)
    return ret + np.where(n < max_exact, n, large)


def _bias_tables(rel_bias, half, mirror=False):
    sgn = -1 if mirror else 1
    i = np.arange(128)[:, None]
    c = np.arange(128)[None, :]
    A = np.empty((128, 8, 3, 512), np.float32)
    for bi, d in enumerate(DIL):
        rel_prev = 64 + i - c
        rel_cur = i - 64 - c
        for h in range(8):
            def tab(rel):
                v = rel_bias[_t5_bucket(sgn * rel * d), h]
                return np.where(np.abs(rel) <= 64, v, NEGB).astype(np.float32)
            tp, tcur = tab(rel_prev), tab(rel_cur)
            A[:, h, bi, 0:128] = tp
            A[:, h, bi, 128:256] = tcur
            first = tcur.copy()
            last = tp.copy()
            if half == 0:
                first[0:64, :] = NEGB
            else:
                last[64:128, :] = NEGB
            A[:, h, bi, 256:384] = first
            A[:, h, bi, 384:512] = last
    B = np.empty((128, 8, 640), np.float32)
    for h in range(8):
        def tabb(rel):
            v = rel_bias[_t5_bucket(sgn * rel), 8 + h]
            return np.where(np.abs(rel) <= 128, v, NEGB).astype(np.float32)
        t0, t1, t2 = tabb(128 + i - c), tabb(i - c), tabb(-128 + i - c)
        B[:, h, 0:128] = t0
        B[:, h, 128:256] = t1
        B[:, h, 256:384] = t2
        first = t2.copy()
        last = t0.copy()
        if half == 0:
            first[:, :] = NEGB
        else:
            last[:, :] = NEGB
        B[:, h, 384:512] = first
        B[:, h, 512:640] = last
    return A.reshape(128, -1), B.reshape(128, -1)


def _vecs(inp):
    v = np.zeros((128, 2 * VL), np.float32)
    for l in range(2):
        o = l * VL
        for name, col in (("norm_ffn1", V_NF1), ("norm_mix", V_NMIX), ("norm_ffn2", V_NF2), ("norm_ple", V_NPLE)):
            v[:, o + col:o + col + 8] = inp[name][l].reshape(8, 128).T
        for name, col in (("q_norm_a", V_QNA), ("k_norm_a", V_KNA), ("q_norm_b", V_QNB), ("k_norm_b", V_KNB)):
            v[:, o + col] = np.tile(inp[name][l], 2)
        for hb in range(8):
            v[:, o + V_SINK + hb] = inp["sink_b"][l, hb]
    return v


def _wqkv_dev(w):
    qa, ka, va, qb, kb, vb = (w[..., 0:512], w[..., 512:1024], w[..., 1024:1536], w[..., 1536:2048],
                              w[..., 2048:2176], w[..., 2176:2304])
    kbsw = np.concatenate([kb[..., 64:128], kb[..., 0:64]], axis=-1)
    return np.ascontiguousarray(np.concatenate([qa, ka, qb, kb, kbsw, va, vb], axis=-1))


def build(stages, first, last, fused=False):
    nc = bass.Bass("TRN2", target_bir_lowering=False)
    P = Prog(nc)
    if fused:
        stages = [('A', 0), ('B', 0), ('A', 1), ('B', 1)]
    layers_A = [l for (s, l) in stages if s == 'A']
    layers_B = [l for (s, l) in stages if s in ('B', 'Batt')]
    need_w = sorted(set(layers_A + layers_B))

    def din(name, shape, dt=F32):
        return nc.dram_tensor(name, shape, dt, kind="ExternalInput").ap()

    def dout(name, shape, dt=F32):
        return nc.dram_tensor(name, shape, dt, kind="ExternalOutput").ap()

    xT_in = din("xT_in", [D, T])
    if fused:
        xT_in1 = din("xT_in1", [D, T])
    vecs_d = din("vecs", [128, 2 * VL])
    ident_d = din("ident", [128, 128])
    W = {}
    for l in need_w:
        if l in layers_A:
            W[("w_in1", l)] = din(f"w_in1_{l}", [D, 2 * DFF])
            W[("w_out1", l)] = din(f"w_out1_{l}", [DFF, D])
            W[("w_qkv", l)] = din(f"w_qkv_{l}", [D, QKVW])
        if l in layers_B:
            W[("w_o", l)] = din(f"w_o_{l}", [D, D])
            W[("w_in2", l)] = din(f"w_in2_{l}", [D, 2 * DFF])
            W[("w_out2", l)] = din(f"w_out2_{l}", [DFF, D])
            W[("w_g", l)] = din(f"w_g_{l}", [D, D])
            W[("w_p", l)] = din(f"w_p_{l}", [256, D])
            W[("pT", l)] = din(f"pT_{l}", [256, T])
            if fused and l == 0:
                W[("pT1", l)] = din(f"pT1_{l}", [256, T])
    if layers_B:
        biasA_d = din("biasA", [128, 8 * 3 * 512])
        biasB_d = din("biasB", [128, 8 * 640])
        if fused:
            biasA1_d = din("biasA1", [128, 8 * 3 * 512])
            biasB1_d = din("biasB1", [128, 8 * 640])
    startB = stages[0][0] in ('B', 'Batt') and not fused
    dbg_att = stages[-1][0] == 'Batt'
    if startB:
        q_in = din("q_in", [D, T], BF16)
        kf_in = din("kf_in", [768, FR], BF16)
        vf_in = din("vf_in", [FR, 640], BF16)
    endA = stages[-1][0] == 'A'
    if fused:
        def dscr(name, shape, dt):
            return nc.dram_tensor(name, shape, dt).ap()
        xs0 = dscr("xs0", [D, T], F32)
        qs0 = dscr("qs0", [D, T], BF16)
        ks = {(l, t): dscr(f"ks_{l}_{t}", [768, FR], BF16) for l in range(2) for t in range(2)}
        vs = {(l, t): dscr(f"vs_{l}_{t}", [FR, 640], BF16) for l in range(2) for t in range(2)}
        kmid = {k_: v_[:, HALO:HALO + T] for k_, v_ in ks.items()}
        vmid = {k_: v_[HALO:HALO + T, :] for k_, v_ in vs.items()}
        kfr, vfr = ks, vs
    if endA or dbg_att:
        q_out = dout("q_out", [D, T], BF16)
        k_out = dout("k_out", [768, T], BF16)
        v_out = dout("v_out", [T, 640], BF16)
    xT_out = dout("xT_out", [D, T])

    with contextlib.ExitStack() as es:
        def sb(name, shape, dt):
            return es.enter_context(nc.sbuf_tensor(name, shape, dt))

        xT = sb("xT", [128, 8, T], F32)
        qS = sb("qS", [128, 8, T], BF16)
        vecs = sb("vecs_s", [128, 2 * VL], F32)
        vecs8 = sb("vecs8", [128, 2 * VL], F32)
        ident = sb("ident_s", [128, 128], BF16)
        ones = sb("ones", [128, 128], BF16)
        bones = sb("bones", [128, 128], BF16)
        epsb = sb("epsb", [128, 1], F32)
        ARENA = 108000
        arena = sb("arena", [128, ARENA // 2], BF16)
        ps = [es.enter_context(nc.psum_tensor(f"ps{k}", [128, 512], F32)) for k in range(8)]

        class Carver:
            def __init__(self):
                self.off = 0

            def take(self, shape, dt):
                n = int(np.prod(shape[1:]))
                nb = n * (4 if dt == F32 else 2)
                nb = (nb + 63) // 64 * 64
                assert self.off + nb <= ARENA, (self.off, nb)
                v = arena[:, self.off // 2:(self.off + nb) // 2]
                self.off += nb
                if dt == F32:
                    v = v.bitcast(F32)
                v = v[:, 0:n]
                if len(shape) == 3:
                    v = v.rearrange("p (a b) -> p a b", a=shape[1])
                elif len(shape) == 4:
                    v = v.rearrange("p (a b c) -> p a b c", a=shape[1], b=shape[2])
                return v

        P.op('dve', lambda e: e.memset(ones[:], 1.0), writes=['ones'])
        P.op('dve', lambda e: e.memset(epsb[:], EPS), writes=['epsb'])
        P.op('dve', lambda e: e.memset(bones[:], 0.0), writes=['bones'])
        P.op('dve', lambda e: e.memset(bones[0:64, 0:64], 1.0), reads=['bones'], writes=['bones'])
        P.op('dve', lambda e: e.memset(bones[64:128, 64:128], 1.0), reads=['bones'], writes=['bones'])
        P.op('pool', lambda e: e.dma_start(out=ident[:], in_=ident_d), writes=['ident'], dma='g_ident')
        P.op('sp', lambda e: e.dma_start(out=vecs[:], in_=vecs_d), writes=['vecs'], dma='g_vecs')
        P.op('dve', lambda e: e.tensor_scalar_mul(out=vecs8[:], in0=vecs[:], scalar1=0.125),
             reads=['vecs'], writes=['vecs8'])
        for l in range(2):
            c0 = l * VL + V_SINK
            P.op('act', lambda e, c0=c0: e.activation(out=vecs8[:, c0:c0 + 8], in_=vecs[:, c0:c0 + 8], func=AF.Exp),
                 reads=['vecs', 'vecs8'], writes=['vecs8'])
        def load_x(src):
            P.op('sp', lambda e: e.dma_start(out=xT[:, :, :], in_=src.rearrange("(dc p) t -> p dc t", p=128)),
                 writes=[('xT', dc, tg) for dc in range(8) for tg in range(NTG)], dma='g_xin')

        def load_q(src):
            P.op('sp', lambda e: e.dma_start(out=qS[:, :, :], in_=src.rearrange("(c p) t -> p c t", p=128)),
                 writes=[('q', c, h) for c in range(8) for h in range(2)], dma='g_qin')

        def save_x(dst, final=False):
            P.op('sp', lambda e: e.dma_start(out=dst.rearrange("(dc p) t -> p dc t", p=128), in_=xT[:, :, :]),
                 reads=[('xT', dc, tg) for dc in range(8) for tg in range(NTG)], dma='g_xout', final=final)

        def save_q(dst, final=False):
            P.op('sp', lambda e: e.dma_start(out=dst.rearrange("(c p) t -> p c t", p=128), in_=qS[:, :, :]),
                 reads=[('q', c, h) for c in range(8) for h in range(2)], dma='g_qout', final=final)

        def make_frames(l, t):
            kd, vd = ks[(l, t)], vs[(l, t)]
            if t == 0:
                kl, vl = kmid[(l, 0)][:, 0:HALO], vmid[(l, 0)][0:HALO, :]
            else:
                kl, vl = kmid[(l, 0)][:, T - HALO:T], vmid[(l, 0)][T - HALO:T, :]
            kr, vr = kmid[(l, 1)][:, 0:HALO], vmid[(l, 1)][0:HALO, :]
            for (dst, src) in ((kd[:, 0:HALO], kl), (kd[:, HALO + T:FR], kr), (vd[0:HALO, :], vl), (vd[HALO + T:FR, :], vr)):
                P.op('sp', lambda e, dst=dst, src=src: e.dma_start(out=dst, in_=src), dma='g_frames')

        if not fused:
            load_x(xT_in)
            if startB:
                load_q(q_in)

        def norm_phase(cv, hn, gcol, tgs=tuple(range(NTG))):
            sq = [cv.take([128, 512], BF16) for _ in range(3)]
            std = [cv.take([128, 512], F32) for _ in range(2)]
            k = 0
            for tg in tgs:
                ts = slice(tg * 512, (tg + 1) * 512)
                for dc in range(8):
                    s = k % 3
                    k += 1
                    P.op('act', lambda e, s=s, dc=dc, ts=ts: e.activation(out=sq[s], in_=xT[:, dc, ts], func=AF.Square,
                                                                          scale=1.0 / 32.0),
                         reads=[('xT', dc, tg)], writes=[('nsq', s)])
                    P.op('pe', lambda e, s=s, dc=dc: e.matmul(ps[7][:], ones[:], sq[s], start=(dc == 0), stop=(dc == 7)),
                         reads=[('nsq', s), 'ones'], writes=[('ps', 7)])
                b = tg % 2
                P.op('act', lambda e, b=b: e.activation(out=std[b], in_=ps[7][:], func=AF.Ln, bias=epsb[:, 0:1], scale=1.0),
                     reads=[('ps', 7), 'epsb'], writes=[('nstd', b)])
                P.op('act', lambda e, b=b: e.activation(out=std[b], in_=std[b], func=AF.Exp, scale=-0.5),
                     reads=[('nstd', b)], writes=[('nstd', b)])
                for dc in range(8):
                    P.op('dve', lambda e, b=b, dc=dc, ts=ts: e.scalar_tensor_tensor(
                        out=hn[:, dc, ts], in0=xT[:, dc, ts], scalar=vecs[:, gcol + dc:gcol + dc + 1], in1=std[b],
                        op0=ALU.mult, op1=ALU.mult),
                        reads=[('xT', dc, tg), ('nstd', b), 'vecs'], writes=[('hn', dc, tg)])

        def ffn_phase(l, w_in, w_out, gcol, tgs=tuple(range(NTG))):
            P.barrier()
            cv = Carver()
            hn = cv.take([128, 8, T], BF16)
            win = [cv.take([128, 8, 1024], BF16) for _ in range(2)]
            wout = [cv.take([128, 4, 1024], BF16) for _ in range(2)]
            act = [cv.take([128, 4, 512], BF16) for _ in range(2)]
            sg = [cv.take([128, 512], F32) for _ in range(2)]
            norm_phase(cv, hn, gcol, tgs)
            parts = [(f0, min(4, NFC - f0)) for f0 in range(0, NFC, 4)]
            w_in_v = w_in.rearrange("(dc p) f -> p dc f", p=128)
            w_out_v = w_out.rearrange("(fc p) d -> p fc d", p=128)

            def load_part(pi):
                f0, nf = parts[pi]
                s = pi % 2
                P.op('pool', lambda e: e.dma_start(out=win[s][:, :, 0:nf * 128], in_=w_in_v[:, :, f0 * 128:(f0 + nf) * 128]),
                     writes=[('wing', s)], dma=f'g_wing{s}')
                P.op('pool', lambda e: e.dma_start(out=win[s][:, :, 512:512 + nf * 128],
                                                   in_=w_in_v[:, :, DFF + f0 * 128:DFF + (f0 + nf) * 128]),
                     writes=[('winu', s)], dma=f'g_winu{s}')
                P.op('pool', lambda e: e.dma_start(out=wout[s][:, 0:nf, :], in_=w_out_v[:, f0:f0 + nf, :]),
                     writes=[('wout', s)], dma=f'g_wout{s}')

            jobs = [(pi, tg) for pi in range(len(parts)) for tg in tgs]
            gk = [0]
            yk = [0]

            def GU(ji):
                pi, tg = jobs[ji]
                f0, nf = parts[pi]
                s = pi % 2
                a = ji % 2
                ts = slice(tg * 512, (tg + 1) * 512)
                for fc in range(nf):
                    b = gk[0] % 2
                    gk[0] += 1
                    for dc in range(8):
                        P.op('pe', lambda e, dc=dc, fc=fc, b=b: e.matmul(ps[b][:], win[s][:, dc, fc * 128:(fc + 1) * 128],
                                                                         hn[:, dc, ts], start=(dc == 0), stop=(dc == 7)),
                             reads=[('wing', s), ('hn', dc, tg)], writes=[('ps', b)])
                    for dc in range(8):
                        P.op('pe', lambda e, dc=dc, fc=fc, b=b: e.matmul(ps[2 + b][:], win[s][:, dc, 512 + fc * 128:512 + (fc + 1) * 128],
                                                                         hn[:, dc, ts], start=(dc == 0), stop=(dc == 7)),
                             reads=[('winu', s), ('hn', dc, tg)], writes=[('ps', 2 + b)])
                    P.op('act', lambda e, b=b: e.activation(out=sg[b], in_=ps[b][:], func=AF.Silu),
                         reads=[('ps', b)], writes=[('sg', b)])
                    P.op('dve', lambda e, b=b, fc=fc: e.tensor_tensor(out=act[a][:, fc, :], in0=ps[2 + b][:], in1=sg[b], op=ALU.mult),
                         reads=[('ps', 2 + b), ('sg', b)], writes=[('act', a, fc)])

            def Y(ji):
                pi, tg = jobs[ji]
                f0, nf = parts[pi]
                s = pi % 2
                a = ji % 2
                ts = slice(tg * 512, (tg + 1) * 512)
                for dc in range(8):
                    b = 4 + yk[0] % 3
                    yk[0] += 1
                    for fc in range(nf):
                        P.op('pe', lambda e, dc=dc, fc=fc, b=b: e.matmul(ps[b][:], wout[s][:, fc, dc * 128:(dc + 1) * 128],
                                                                         act[a][:, fc, :], start=(fc == 0), stop=(fc == nf - 1)),
                             reads=[('wout', s), ('act', a, fc)], writes=[('ps', b)])
                    P.op('dve', lambda e, dc=dc, b=b: e.scalar_tensor_tensor(out=xT[:, dc, ts], in0=ps[b][:], scalar=0.5,
                                                                            in1=xT[:, dc, ts], op0=ALU.mult, op1=ALU.add),
                         reads=[('ps', b), ('xT', dc, tg)], writes=[('xT', dc, tg)])

            load_part(0)
            load_part(1)
            for ji in range(len(jobs) + 1):
                if ji < len(jobs):
                    pi, tg = jobs[ji]
                    GU(ji)
                if ji >= 1:
                    Y(ji - 1)
                    ppi, ptg = jobs[ji - 1]
                    if ptg == tgs[-1] and ppi + 2 < len(parts):
                        load_part(ppi + 2)

        def qkv_phase(l, k_out, v_out, final=True, tgs=tuple(range(NTG))):
            P.barrier()
            cv = Carver()
            hn = cv.take([128, 8, T], BF16)
            wq = cv.take([128, 8, QKVW], BF16)
            kst = [cv.take([128, T], BF16) for _ in range(2)]
            vst = [cv.take([128, 640], BF16) for _ in range(2)]
            sq = [cv.take([128, 512], BF16) for _ in range(5)]
            std = [cv.take([128, 512], F32) for _ in range(5)]
            wv = W[("w_qkv", l)].rearrange("(dc p) f -> p dc f", p=128)
            for (c0, c1) in ((0, 1024), (1024, 1792), (1792, QKVW)):
                P.op('pool', lambda e, c0=c0, c1=c1: e.dma_start(out=wq[:, :, c0:c1], in_=wv[:, :, c0:c1]),
                     writes=[('wq', c0)], dma=f'g_wq{c0}')
            norm_phase(cv, hn, l * VL + V_NMIX, tgs)
            vo = l * VL
            units = [(ch, tg) for ch in range(14) for tg in tgs]
            ncol = 512 * len(tgs)

            def unit_info(ch):
                if ch < 4:
                    return 'q', vecs8[:, vo + V_QNA:vo + V_QNA + 1], ch
                elif ch < 8:
                    return 'k', vecs[:, vo + V_KNA:vo + V_KNA + 1], ch - 4
                elif ch < 12:
                    return 'q', vecs8[:, vo + V_QNB:vo + V_QNB + 1], ch - 4
                return 'k', vecs[:, vo + V_KNB:vo + V_KNB + 1], ch - 8

            def proj(ui):
                ch, tg = units[ui]
                b = ui % 5
                ts = slice(tg * 512, (tg + 1) * 512)
                wres = ('wq', 0) if ch < 8 else ('wq', 1024)
                for dc in range(8):
                    P.op('pe', lambda e, dc=dc: e.matmul(ps[b][:], wq[:, dc, ch * 128:(ch + 1) * 128], hn[:, dc, ts],
                                                         start=(dc == 0), stop=(dc == 7)),
                         reads=[wres, ('hn', dc, tg)], writes=[('ps', b)])
                P.op('act', lambda e: e.activation(out=sq[b], in_=ps[b][:], func=AF.Square, scale=0.125),
                     reads=[('ps', b)], writes=[('qsq', b)])

            def stats(ui):
                ch, tg = units[ui]
                b = ui % 5
                sb_ = 5 + ui % 2
                ts = slice(tg * 512, (tg + 1) * 512)
                kind, gv, dst = unit_info(ch)
                P.op('pe', lambda e: e.matmul(ps[sb_][:], bones[:], sq[b], start=True, stop=True),
                     reads=[('qsq', b), 'bones'], writes=[('ps', sb_)])
                P.op('act', lambda e: e.activation(out=std[b], in_=ps[sb_][:], func=AF.Ln, bias=epsb[:, 0:1], scale=1.0),
                     reads=[('ps', sb_), 'epsb'], writes=[('qstd', b)])
                P.op('act', lambda e: e.activation(out=std[b], in_=std[b], func=AF.Exp, scale=-0.5),
                     reads=[('qstd', b)], writes=[('qstd', b)])
                if kind == 'q':
                    P.op('dve', lambda e: e.scalar_tensor_tensor(
                        out=qS[:, dst, ts], in0=ps[b][:], scalar=gv, in1=std[b], op0=ALU.mult, op1=ALU.mult),
                        reads=[('ps', b), ('qstd', b), 'vecs', 'vecs8'], writes=[('q', dst, 0), ('q', dst, 1)])
                else:
                    ksl = dst % 2
                    P.op('dve', lambda e: e.scalar_tensor_tensor(
                        out=kst[ksl][:, ts], in0=ps[b][:], scalar=gv, in1=std[b], op0=ALU.mult, op1=ALU.mult),
                        reads=[('ps', b), ('qstd', b), 'vecs'], writes=[('kst', ksl)])
                    if tg == tgs[-1]:
                        P.op('sp', lambda e: e.dma_start(out=k_out[dst * 128:(dst + 1) * 128, 0:ncol], in_=kst[ksl][:, 0:ncol]),
                             reads=[('kst', ksl)], dma=f'g_kout{ksl}', final=final)

            LAQ = 3
            for ui in range(len(units) + LAQ):
                if ui < len(units):
                    proj(ui)
                if ui >= LAQ:
                    stats(ui - LAQ)
            for tt in range(4 * len(tgs)):
                tsl = slice(tt * 128, (tt + 1) * 128)
                b = tt % 2
                for dc in range(8):
                    P.op('pe', lambda e, dc=dc, b=b, tsl=tsl: e.matmul(ps[4 + b][:], hn[:, dc, tsl], wq[:, dc, 1792:2304],
                                                                       start=(dc == 0), stop=(dc == 7)),
                         reads=[('wq', 1792), ('hn', dc, tt // 4)], writes=[('ps', 4 + b)])
                for dc in range(8):
                    P.op('pe', lambda e, dc=dc, b=b, tsl=tsl: e.matmul(ps[6 + b][:, 0:128], hn[:, dc, tsl], wq[:, dc, 2304:2432],
                                                                       start=(dc == 0), stop=(dc == 7)),
                         reads=[('wq', 1792), ('hn', dc, tt // 4)], writes=[('ps', 6 + b)])
                P.op('act', lambda e, b=b: e.copy(out=vst[b][:, 0:512], in_=ps[4 + b][:]),
                     reads=[('ps', 4 + b)], writes=[('vst', b)])
                P.op('dve', lambda e, b=b: e.tensor_copy(out=vst[b][:, 512:640], in_=ps[6 + b][:, 0:128]),
                     reads=[('ps', 6 + b)], writes=[('vstb', b)])
                P.op('sp', lambda e, b=b, tsl=tsl: e.dma_start(out=v_out[tsl, :], in_=vst[b]),
                     reads=[('vst', b), ('vstb', b)], dma=f'g_vout{b}', final=final)

        def att_phase(l, kf_d, vf_d, biasA_d, biasB_d, qhalf=False):
            P.barrier()
            cv = Carver()
            kfb = [cv.take([128, FR], BF16) for _ in range(2)]
            NTA = 17 + 20 + 32
            VtA = cv.take([128, NTA, 2, 128], BF16)
            VtB = cv.take([128, 18, 2, 128], BF16)
            biasb = [cv.take([128, 3072], BF16) for _ in range(2)]
            PT = {bk: cv.take([128, 512], BF16) for bk in (0, 1, 2, 5, 6, 7)}
            SBANKS = (0, 1, 2, 5, 6, 7)
            Uacc = [cv.take([128, T], F32) for _ in range(2)]
            rz = cv.take([128, T], F32)
            vo = l * VL
            P.op('dve', lambda e: e.memset(VtA[:, :, :, 64:128], 1.0), writes=['Vones'])
            P.op('dve', lambda e: e.memset(VtB[:, :, :, 64:128], 1.0), reads=['Vones'], writes=['Vones'])
            tbase = {}
            n = 0
            for bi, d in enumerate(DIL):
                nqb = 16 // d
                for r in range(d):
                    tbase[(bi, r)] = n
                    n += nqb + 1
            assert n == NTA
            sk = [0]
            ok = [0]

            stream = []

            def run_head(ch, hp, h, jobs, Ua, vt, ku, bs, pre=(), post=()):
                for job in jobs:
                    job.update(ch=ch, hp=hp, Ua=Ua, vtens=vt, ku=ku, bs=bs, pre=[], post=[])
                jobs[0]['pre'] = list(pre)
                jobs[-1]['post'] = list(post)
                stream.extend(jobs)

            def S(job):
                b = SBANKS[sk[0] % 6]
                sk[0] += 1
                job['sb'] = b
                N = job['N']
                ch, hp, ku, bs = job['ch'], job['hp'], job['ku'], job['bs']
                bias = biasb[bs]
                P.op('pe', lambda e: e.matmul(ps[b][:, 0:N], job['k'], job['q'], start=True, stop=True),
                     reads=[('kf', ku), ('q', ch, hp)], writes=[('ps', b)])
                P.op('act', lambda e: e.activation(out=PT[b][:, 0:N], in_=ps[b][:, 0:N], func=AF.Exp),
                     reads=[('ps', b)], writes=[('PT', b)])
                P.op('dve', lambda e: e.tensor_tensor(out=PT[b][:, 0:N], in0=PT[b][:, 0:N], in1=bias[:, job['bc']:job['bc'] + N], op=ALU.mult),
                     reads=[('PT', b), ('bias', bs)], writes=[('PT', b)])

            def PV(job):
                b = job['sb']
                hp, vt = job['hp'], job['vtens']
                blks = job['blocks']
                (col0, grp, slot0, st, _sp, _f) = blks[0]
                nb = len(blks)
                for k_, bl in enumerate(blks):
                    assert bl[2] == slot0 + k_ and bl[0] == col0 + 128 * k_
                ob = 3 + grp % 2
                fin = blks[-1][5]
                P.op('pe', lambda e: e.matmul(
                    ps[ob][:, slot0 * 128:(slot0 + nb) * 128], vt[:, job['vt'], job['vh'], :],
                    PT[b][:, col0:col0 + 128 * nb], start=st, stop=True, skip_group_check=True),
                    reads=[('PT', b), job['vres'], 'Vones'], writes=[('ps', ob)])
                if fin is not None:
                    dst, src_ap, first = fin
                    if first:
                        P.op('dve', lambda e: e.tensor_copy(out=dst, in_=src_ap(ps[ob])),
                             reads=[('ps', ob)], writes=[('Ua', hp)])
                    else:
                        P.op('dve', lambda e: e.tensor_tensor(out=dst, in0=src_ap(ps[ob]), in1=dst, op=ALU.add),
                             reads=[('ps', ob), ('Ua', hp)], writes=[('Ua', hp)])

            def flush_stream():
                LA = 5
                DEFER = 6
                pending = []
                for i in range(len(stream) + LA):
                    if i < len(stream):
                        for f in stream[i]['pre']:
                            f()
                        S(stream[i])
                    if i >= LA:
                        job = stream[i - LA]
                        PV(job)
                        for f in job['post']:
                            f()
                        for k_, f in enumerate(job.get('fin', ())):
                            pending.append((i + DEFER + k_, f))
                    if pending and pending[0][0] <= i:
                        pending.pop(0)[1]()
                for (_r, f) in pending:
                    f()

            gcount = [0]

            def make_jobs_A(ch, hp, Ua, kf):
                jobs = []
                qv = qS[hp * 64:(hp + 1) * 64, ch, :]
                kv = kf[hp * 64:(hp + 1) * 64, :]
                bh = hp * 1536
                for bi, d in enumerate(DIL):
                    nqb = 16 // d
                    if d == 16:
                        groups = [[(r, 0) for r in range(g * 4, g * 4 + 4)] for g in range(4)]
                    elif d == 4:
                        groups = [[(r, b) for b in range(2 if qhalf else 4)] for r in range(4)]
                    else:
                        groups = [[(0, b) for b in range(g * 4, g * 4 + 4)] for g in range(2 if qhalf else 4)]
                    for grp_blocks in groups:
                        grp = gcount[0]
                        gcount[0] += 1
                        slot_of = {rb: s for s, rb in enumerate(grp_blocks)}
                        classes = sorted(set(r for r, _ in grp_blocks))
                        if d == 1:
                            b0 = grp_blocks[0][1]
                            dst = Ua[:, b0 * 128:b0 * 128 + 512]
                            src_ap = (lambda p: p[:, 0:512])
                        elif d == 4:
                            r = grp_blocks[0][0]
                            nm = 128 * len(grp_blocks)
                            dst = Ua[:, :].rearrange("p (m r) -> p r m", r=4)[:, r, 0:nm]
                            src_ap = (lambda p, nm=nm: p[:, 0:nm])
                        else:
                            r0 = grp_blocks[0][0]
                            dst = Ua[:, :].rearrange("p (m r) -> p r m", r=16)[:, r0:r0 + 4, :]
                            src_ap = (lambda p: p[:, 0:512].rearrange("p (r m) -> p r m", r=4))
                        first = (bi == 0)
                        pending = []
                        for r in classes:
                            blks = sorted(b for (rr, b) in grp_blocks if rr == r)
                            tiles = sorted(set([b for b in blks] + [b + 1 for b in blks]))
                            for j in tiles:
                                served = [b for b in (j - 1, j) if b in blks]
                                bq0 = served[0]
                                N = 128 * len(served)
                                kstart = HALO + d * (128 * j - 64) + r
                                kap = kv.rearrange("p (m r) -> p r m", r=d)[:, kstart % d, kstart // d:kstart // d + 128]
                                qstart = d * 128 * bq0 + r
                                qap = qv.rearrange("p (m r) -> p r m", r=d)[:, qstart % d, qstart // d:qstart // d + N]
                                if j == 0:
                                    bc = bh + bi * 512 + 256
                                elif j == nqb:
                                    bc = bh + bi * 512 + 384
                                else:
                                    bc = bh + bi * 512 + (0 if served[0] == j - 1 else 128)
                                blocks = []
                                for si, b in enumerate(served):
                                    st = (len(pending) == 0)
                                    pending.append(1)
                                    sp_ = (j == b + 1)
                                    blocks.append([si * 128, grp, slot_of[(r, b)], st, sp_, None])
                                jobs.append({'k': kap, 'q': qap, 'N': N, 'bc': bc, 'vt': tbase[(bi, r)] + j, 'vh': hp,
                                             'blocks': blocks, 'vres': ('Vt', bi, hp)})
                        jobs[-1]['blocks'][-1][5] = (dst, src_ap, first)
                return jobs

            def make_jobs_B(ch, hp, Ua, kvh, kf):
                jobs = []
                qv = qS[hp * 64:(hp + 1) * 64, ch, :]
                kv = kf[hp * 64:(hp + 1) * 64, :]
                bh = hp * 640
                for g in range(2 if qhalf else 4):
                    grp = gcount[0]
                    gcount[0] += 1
                    blks = list(range(g * 4, g * 4 + 4))
                    tiles = list(range(blks[0], blks[-1] + 3))
                    for j in tiles:
                        served = [b for b in (j - 2, j - 1, j) if b in blks]
                        N = 128 * len(served)
                        kstart = HALO + 128 * (j - 1)
                        kap = kv[:, kstart:kstart + 128]
                        qap = qv[:, served[0] * 128:served[0] * 128 + N]
                        if j == 0:
                            bc = bh + 384
                        elif j == 17:
                            bc = bh + 512
                        else:
                            bc = bh + 128 * (served[0] - (j - 2))
                        blocks = []
                        for si, b in enumerate(served):
                            blocks.append([si * 128, grp, b - blks[0], (j == tiles[0] and si == 0), (j == b + 2), None])
                        jobs.append({'k': kap, 'q': qap, 'N': N, 'bc': bc, 'vt': j, 'vh': kvh, 'blocks': blocks, 'vres': ('VtB', kvh)})
                    dst = Ua[:, blks[0] * 128:blks[0] * 128 + 512]
                    jobs[-1]['blocks'][-1][5] = (dst, (lambda p: p[:, 0:512]), True)
                return jobs

            def finish_head(ch, hp, Ua, sinkcol):
                hs = slice(hp * 64, (hp + 1) * 64)
                nq = T // 2 if qhalf else T
                pieces = []
                if sinkcol is not None:
                    pieces.append(lambda: P.op('act', lambda e: e.activation(out=rz[64:128, 0:nq], in_=Ua[64:128, 0:nq], func=AF.Ln,
                                                                             bias=vecs8[64:128, sinkcol:sinkcol + 1], scale=1.0),
                                               reads=[('Ua', hp), 'vecs8'], writes=[('rz',)]))
                else:
                    pieces.append(lambda: P.op('act', lambda e: e.activation(out=rz[64:128, 0:nq], in_=Ua[64:128, 0:nq], func=AF.Ln),
                                               reads=[('Ua', hp)], writes=[('rz',)]))
                pieces.append(lambda: P.op('act', lambda e: e.activation(out=rz[64:128, 0:nq], in_=rz[64:128, 0:nq], func=AF.Exp, scale=-1.0),
                                           reads=[('rz',)], writes=[('rz',)]))
                nchunk = 4
                w = nq // nchunk
                for k_ in range(nchunk):
                    cs = slice(k_ * w, (k_ + 1) * w)
                    pieces.append(lambda cs=cs, k_=k_: P.op('dve', lambda e: e.tensor_copy(out=rz[0:64, cs], in_=rz[64:128, cs]),
                                                            reads=[('rz',)], writes=[('rzs', k_)]))
                    pieces.append(lambda cs=cs, k_=k_: P.op('dve', lambda e: e.tensor_tensor(out=qS[hs, ch, cs], in0=Ua[0:64, cs], in1=rz[0:64, cs], op=ALU.mult),
                                                            reads=[('Ua', hp), ('rzs', k_)], writes=[('q', ch, hp)]))
                return pieces

            units = [('A', ch, None) for ch in range(4)] + [('B', ch, hp) for ch in range(4, 8) for hp in range(2)]

            def kf_bias_dma(u):
                kind, ch, hp = units[u]
                ku = u % 2
                if kind == 'A':
                    P.op('sp', lambda e: e.dma_start(out=kfb[ku][:, :], in_=kf_d[ch * 128:(ch + 1) * 128, :]),
                         writes=[('kf', ku)], dma=f'g_kf{ku}')
                    bs = ch % 2
                    P.op('pool', lambda e: e.dma_start(out=biasb[bs][:, 0:3072], in_=biasA_d[:, ch * 3072:(ch + 1) * 3072]),
                         writes=[('biasraw', bs), ('bias', bs)], dma=f'g_bias{bs}')
                else:
                    hb = (ch - 4) * 2 + hp
                    kvh = hb // 4
                    krow = 512 if kvh == hp else 640
                    P.op('sp', lambda e: e.dma_start(out=kfb[ku][:, :], in_=kf_d[krow:krow + 128, :]),
                         writes=[('kf', ku)], dma=f'g_kf{ku}')
                    if hp == 0:
                        bs = ch % 2
                        P.op('pool', lambda e: e.dma_start(out=biasb[bs][:, 0:1280], in_=biasB_d[:, (ch - 4) * 1280:(ch - 4 + 1) * 1280]),
                             writes=[('biasraw', bs), ('bias', bs)], dma=f'g_bias{bs}')

            def bias_exp(u):
                kind, ch, hp = units[u]
                if kind == 'B' and hp == 1:
                    return
                bs = ch % 2
                n = 3072 if kind == 'A' else 1280
                P.op('act', lambda e: e.activation(out=biasb[bs][:, 0:n], in_=biasb[bs][:, 0:n], func=AF.Exp),
                     reads=[('biasraw', bs)], writes=[('bias', bs)])

            def v_loads(ch):
                for hh in range(2):
                    for bi, d in enumerate(DIL):
                        nqb = 16 // d
                        for r in range(d):
                            for tlo, thi in ([(0, 9), (9, 17)] if d == 1 else [(0, nqb + 1)]):
                                nt = thi - tlo
                                row0 = HALO + d * (128 * tlo - 64) + r
                                t0 = tbase[(bi, r)] + tlo
                                src = bass.AP(vf_d.tensor, row0 * 640 + ch * 128 + hh * 64,
                                              [[d * 640, 128], [d * 128 * 640, nt], [1, 64]])
                                P.op('sp', lambda e, src=src, t0=t0, nt=nt, hh=hh: e.dma_start(out=VtA[:, t0:t0 + nt, hh, 0:64], in_=src),
                                     also_writes=[('Vt', bi, hh)], dma=f'g_vt{bi}{hh}')

            for hh in range(2):
                src = bass.AP(vf_d.tensor, (HALO - 128) * 640 + 512 + hh * 64, [[640, 128], [128 * 640, 18], [1, 64]])
                P.op('sp', lambda e, src=src, hh=hh: e.dma_start(out=VtB[:, :, hh, 0:64], in_=src), writes=[('VtB', hh)], dma=f'g_vtb{hh}')
            kf_bias_dma(0)
            v_loads(0)
            for u, (kind, ch, hp_) in enumerate(units):
                pre = []
                if u + 1 < len(units):
                    pre.append(lambda u=u: kf_bias_dma(u + 1))
                pre.append(lambda u=u: bias_exp(u))
                ku = u % 2
                bs = ch % 2
                if kind == 'A':
                    for hp in range(2):
                        Ua = Uacc[hp]
                        jobs = make_jobs_A(ch, hp, Ua, kfb[ku])
                        post = []
                        if hp == 1 and ch + 1 < 4:
                            post.append(lambda ch=ch: v_loads(ch + 1))
                        run_head(ch, hp, None, jobs, Ua, VtA, ku, bs, pre=(pre if hp == 0 else ()), post=post)
                        jobs[-1]['fin'] = finish_head(ch, hp, Ua, None)
                else:
                    hp = hp_
                    Ua = Uacc[hp]
                    hb = (ch - 4) * 2 + hp
                    kvh = hb // 4
                    jobs = make_jobs_B(ch, hp, Ua, kvh, kfb[ku])
                    run_head(ch, hp, None, jobs, Ua, VtB, ku, bs, pre=pre, post=[])
                    jobs[-1]['fin'] = finish_head(ch, hp, Ua, vo + V_SINK + hb)
            flush_stream()

        def wo_phase(l, tgs=tuple(range(NTG))):
            P.barrier()
            cv = Carver()
            wo = cv.take([128, 8, D], BF16)
            wv = W[("w_o", l)].rearrange("(c p) d -> p c d", p=128)
            for h in range(2):
                P.op('pool', lambda e, h=h: e.dma_start(out=wo[:, 4 * h:4 * h + 4, :], in_=wv[:, 4 * h:4 * h + 4, :]),
                     writes=[('wo', h)], dma=f'g_wo{h}')
            k = 0
            for tg in tgs:
                ts = slice(tg * 512, (tg + 1) * 512)
                for dc in range(8):
                    b = k % 4
                    k += 1
                    for c in range(8):
                        P.op('pe', lambda e, c=c, dc=dc, b=b, ts=ts: e.matmul(ps[b][:], wo[:, c, dc * 128:(dc + 1) * 128], qS[:, c, ts],
                                                                              start=(c == 0), stop=(c == 7)),
                             reads=[('wo', c // 4), ('q', c, 0), ('q', c, 1)], writes=[('ps', b)])
                    P.op('dve', lambda e, dc=dc, b=b, ts=ts: e.tensor_tensor(out=xT[:, dc, ts], in0=ps[b][:], in1=xT[:, dc, ts], op=ALU.add),
                         reads=[('ps', b), ('xT', dc, tg)], writes=[('xT', dc, tg)])

        def ple_phase(l, pT_d, tgs=tuple(range(NTG))):
            P.barrier()
            cv = Carver()
            hn = cv.take([128, 8, T], BF16)
            wg = cv.take([128, 8, D], BF16)
            wp = cv.take([128, 2, D], BF16)
            pT = cv.take([128, 2, T], BF16)
            sgm = [cv.take([128, 512], F32) for _ in range(2)]
            wgv = W[("w_g", l)].rearrange("(c p) d -> p c d", p=128)
            wpv = W[("w_p", l)].rearrange("(c p) d -> p c d", p=128)
            ptv = pT_d.rearrange("(c p) t -> p c t", p=128)
            for h in range(2):
                P.op('pool', lambda e, h=h: e.dma_start(out=wg[:, 4 * h:4 * h + 4, :], in_=wgv[:, 4 * h:4 * h + 4, :]),
                     writes=[('wg', h)], dma=f'g_wg{h}')
            P.op('pool', lambda e: e.dma_start(out=wp[:, :, :], in_=wpv), writes=[('wp',)], dma='g_wp')
            P.op('pool', lambda e: e.dma_start(out=pT[:, :, :], in_=ptv), writes=[('pT',)], dma='g_pT')
            norm_phase(cv, hn, l * VL + V_NPLE, tgs)
            k = 0
            for tg in tgs:
                ts = slice(tg * 512, (tg + 1) * 512)
                for dc in range(8):
                    b = k % 2
                    k += 1
                    for c in range(8):
                        P.op('pe', lambda e, c=c, dc=dc, b=b, ts=ts: e.matmul(ps[b][:], wg[:, c, dc * 128:(dc + 1) * 128], hn[:, c, ts],
                                                                              start=(c == 0), stop=(c == 7)),
                             reads=[('wg', c // 4), ('hn', c, tg)], writes=[('ps', b)])
                    for c in range(2):
                        P.op('pe', lambda e, c=c, dc=dc, b=b, ts=ts: e.matmul(ps[2 + b][:], wp[:, c, dc * 128:(dc + 1) * 128], pT[:, c, ts],
                                                                              start=(c == 0), stop=(c == 1)),
                             reads=[('wp',), ('pT',)], writes=[('ps', 2 + b)])
                    P.op('act', lambda e, b=b: e.activation(out=sgm[b], in_=ps[b][:], func=AF.Sigmoid),
                         reads=[('ps', b)], writes=[('sgm', b)])
                    P.op('dve', lambda e, b=b: e.tensor_tensor(out=sgm[b], in0=ps[2 + b][:], in1=sgm[b], op=ALU.mult),
                         reads=[('ps', 2 + b), ('sgm', b)], writes=[('sgm', b)])
                    P.op('dve', lambda e, b=b, dc=dc, ts=ts: e.tensor_tensor(out=xT[:, dc, ts], in0=sgm[b], in1=xT[:, dc, ts], op=ALU.add),
                         reads=[('sgm', b), ('xT', dc, tg)], writes=[('xT', dc, tg)])

        ALLTG = tuple(range(NTG))

        def stage_A(l, k_dst, v_dst, final, tgs=ALLTG):
            ffn_phase(l, W[("w_in1", l)], W[("w_out1", l)], l * VL + V_NF1, tgs)
            qkv_phase(l, k_dst, v_dst, final, tgs)

        def stage_B(l, kf_d, vf_d, bA, bB, pT_d, tgs=ALLTG):
            att_phase(l, kf_d, vf_d, bA, bB, qhalf=(len(tgs) < NTG))
            wo_phase(l, tgs)
            ffn_phase(l, W[("w_in2", l)], W[("w_out2", l)], l * VL + V_NF2, tgs)
            ple_phase(l, pT_d, tgs)

        if not fused:
            for (s, l) in stages:
                if s == 'A':
                    stage_A(l, k_out, v_out, True)
                elif s == 'Batt':
                    att_phase(l, kf_in, vf_in, biasA_d, biasB_d)
                else:
                    stage_B(l, kf_in, vf_in, biasA_d, biasB_d, W[("pT", l)])
            P.barrier()
            save_x(xT_out, final=True)
            if endA or dbg_att:
                save_q(q_out, final=True)
        else:
            load_x(xT_in)
            stage_A(0, kmid[(0, 0)], vmid[(0, 0)], False)
            P.barrier()
            save_x(xs0)
            save_q(qs0)
            P.barrier()
            load_x(xT_in1)
            stage_A(0, kmid[(0, 1)], vmid[(0, 1)], False)
            P.barrier()
            make_frames(0, 1)
            make_frames(0, 0)
            stage_B(0, kfr[(0, 1)], vfr[(0, 1)], biasA1_d, biasB1_d, W[("pT1", 0)], tgs=(0, 1))
            stage_A(1, kmid[(1, 1)], vmid[(1, 1)], False, tgs=(0, 1))
            P.barrier()
            load_x(xs0)
            load_q(qs0)
            stage_B(0, kfr[(0, 0)], vfr[(0, 0)], biasA_d, biasB_d, W[("pT", 0)])
            stage_A(1, kmid[(1, 0)], vmid[(1, 0)], False)
            P.barrier()
            make_frames(1, 0)
            stage_B(1, kfr[(1, 0)], vfr[(1, 0)], biasA_d, biasB_d, W[("pT", 1)])
            P.barrier()
            save_x(xT_out, final=True)
        P.emit()
    return nc


_PROGS = {}


def _get_prog(key, stages, first, last, fused=False):
    if key not in _PROGS:
        _PROGS[key] = build(stages, first, last, fused=fused)
    return _PROGS[key]


def _frames(k_out, v_out):
    kfs, vfs = [], []
    for c in range(NCORES):
        half = c % 2
        kf = np.zeros((768, FR), k_out[c].dtype)
        vf = np.zeros((FR, 640), v_out[c].dtype)
        kf[:, HALO:HALO + T] = k_out[c]
        vf[HALO:HALO + T] = v_out[c]
        if half == 0:
            kf[:, HALO + T:] = k_out[c + 1][:, 0:HALO]
            vf[HALO + T:] = v_out[c + 1][0:HALO]
        else:
            kf[:, 0:HALO] = k_out[c - 1][:, T - HALO:T]
            vf[0:HALO] = v_out[c - 1][T - HALO:T]
        kfs.append(kf)
        vfs.append(vf)
    return kfs, vfs


def kernel(**inp):
    inp = {k: np.asarray(v) for k, v in inp.items()}
    x, p = inp["x"], inp["p"]
    vecs = _vecs(inp)
    wqkv = _wqkv_dev(inp["w_qkv"])
    tabs = [[_bias_tables(inp["rel_bias"], t, mirror=bool(m)) for t in range(2)] for m in range(2)]
    identh = np.eye(128, dtype=np.float32)
    cores = list(range(NCORES))
    nc = _get_prog("fused", None, True, True, fused=True)
    maps = []
    S = 2 * T
    for c in cores:
        b, half = c // 2, c % 2
        order = np.arange(S) if half == 0 else np.arange(S - 1, -1, -1)
        own, oth = order[0:T], order[T:S]
        tb = tabs[half]
        m = dict(xT_in=np.ascontiguousarray(x[b, own, :].T), xT_in1=np.ascontiguousarray(x[b, oth, :].T),
                 vecs=vecs, ident=identh,
                 biasA=tb[0][0], biasB=tb[0][1], biasA1=tb[1][0], biasB1=tb[1][1],
                 pT_0=np.ascontiguousarray(p[0, b, own, :].T), pT1_0=np.ascontiguousarray(p[0, b, oth, :].T),
                 pT_1=np.ascontiguousarray(p[1, b, own, :].T))
        for l in range(2):
            m.update({f"w_in1_{l}": inp["ffn1_w_in"][l], f"w_out1_{l}": inp["ffn1_w_out"][l], f"w_qkv_{l}": wqkv[l],
                      f"w_o_{l}": inp["w_o"][l], f"w_in2_{l}": inp["ffn2_w_in"][l], f"w_out2_{l}": inp["ffn2_w_out"][l],
                      f"w_g_{l}": inp["w_ple_gate"][l], f"w_p_{l}": inp["w_ple_proj"][l]})
        maps.append(m)
    res = run_bass_kernel_spmd(nc, maps, core_ids=cores).results
    out = np.empty((4, S, D), np.float32)
    for c in cores:
        b, half = c // 2, c % 2
        order = np.arange(S) if half == 0 else np.arange(S - 1, -1, -1)
        out[b, order[0:T], :] = res[c]["xT_out"].T
    return out
```
